# Optimizing a Trainium2 kernel written in Bass

```python
import jax, jax.numpy as jnp
from jax import lax
import numpy as np

D_MODEL = 1024
BATCH = 4
SEQ = 4096
DEPTH = 2
DEC_BATCH = 8
DEC_SEQ = 32
PAST_LEN = 2048

CHUNK = 64
HEAD_DIM = 64
A_HEADS = D_MODEL // HEAD_DIM
A_KV_HEADS = A_HEADS // 4
A_GROUP = A_HEADS // A_KV_HEADS
A_WINDOW = 128
A_PREV = A_WINDOW // CHUNK
B_HEADS = D_MODEL // HEAD_DIM
B_PREV = 8
B_REACH = B_PREV * CHUNK
REL_CLIP = 128
N_REL = 2 * REL_CLIP + 1
D_FF = 2816
CONV_W = 3
EPS = 1e-6
NEG_INF = -1e30
A_Q = A_HEADS * HEAD_DIM
A_KV = A_KV_HEADS * HEAD_DIM
B_W = B_HEADS * HEAD_DIM
N_IN = A_Q + 2 * A_KV + 3 * B_W + 2 * D_MODEL
SPLITS = (A_Q, A_Q + A_KV, A_Q + 2 * A_KV, A_Q + 2 * A_KV + B_W,
          A_Q + 2 * A_KV + 2 * B_W, A_Q + 2 * A_KV + 3 * B_W,
          A_Q + 2 * A_KV + 3 * B_W + D_MODEL)

kernel_name = 'hybrid_swa_sink_chunkband_convffn_step'


def rmsnorm(x, g):
    xf = x.astype(jnp.float32)
    y = xf * lax.rsqrt(jnp.mean(xf * xf, axis=-1, keepdims=True) + EPS)
    return (y * g.astype(jnp.float32)).astype(x.dtype)


def rel_dist(n_q, n_k, n_before):
    return jnp.arange(n_q)[:, None] + n_before - jnp.arange(n_k)[None, :]


def alibi_bias(dist):
    slopes = 2.0 ** (-8.0 * jnp.arange(1, A_HEADS + 1, dtype=jnp.float32) / A_HEADS)
    b = -slopes[:, None, None] * jnp.abs(dist).astype(jnp.float32)[None]
    return b.reshape(A_KV_HEADS, A_GROUP, dist.shape[0], dist.shape[1])


def rel_bias(table, dist):
    idx = jnp.clip(dist, -REL_CLIP, REL_CLIP) + REL_CLIP
    return jnp.transpose(table[idx].astype(jnp.float32), (2, 0, 1))[:, None]


def attend(q, k, v, bias, sink):
    s = jnp.einsum('bqngd,bknd->bngqk', q, k, preferred_element_type=jnp.float32) * (HEAD_DIM ** -0.5) + bias
    if sink is not None:
        col = jnp.broadcast_to(sink.astype(jnp.float32)[:, :, None, None], s.shape[:-1] + (1,))
        p = jax.nn.softmax(jnp.concatenate([s, col], axis=-1), axis=-1)[..., :-1]
    else:
        p = jax.nn.softmax(s, axis=-1)
    return jnp.einsum('bngqk,bknd->bqngd', p.astype(v.dtype), v)


def band_prompt(q, k, v, n_prev, bias, sink):
    b, s = q.shape[:2]
    n_chunks = s // CHUNK
    reach = n_prev * CHUNK
    span = reach + CHUNK
    pad = ((0, 0), (reach, 0), (0, 0), (0, 0))
    kp, vp = jnp.pad(k, pad), jnp.pad(v, pad)
    qc = jnp.swapaxes(q.reshape((b, n_chunks, CHUNK) + q.shape[2:]), 0, 1)

    def one_chunk(args):
        c, qi = args
        start = c * CHUNK
        kb = lax.dynamic_slice_in_dim(kp, start, span, axis=1)
        vb = lax.dynamic_slice_in_dim(vp, start, span, axis=1)
        valid = start - reach + jnp.arange(span) >= 0
        return attend(qi, kb, vb, jnp.where(valid, bias, NEG_INF), sink)

    out = lax.map(one_chunk, (jnp.arange(n_chunks), qc))
    return jnp.swapaxes(out, 0, 1).reshape(b, s, -1)


def project(h, w_in, b_gate, qa_g, ka_g, qb_g, kb_g):
    b, t = h.shape[:2]
    z = h @ w_in
    qa, ka, va, qb, kb, vb, ga, gb = jnp.split(z, SPLITS, axis=-1)
    heads = lambda u, n: u.reshape(b, t, n, HEAD_DIM)
    qa = rmsnorm(heads(qa, A_HEADS), qa_g).reshape(b, t, A_KV_HEADS, A_GROUP, HEAD_DIM)
    ka = rmsnorm(heads(ka, A_KV_HEADS), ka_g)
    va = heads(va, A_KV_HEADS)
    qb = rmsnorm(heads(qb, B_HEADS), qb_g)[:, :, :, None]
    kb = rmsnorm(heads(kb, B_HEADS), kb_g)
    vb = heads(vb, B_HEADS)
    ga = jax.nn.sigmoid(ga + b_gate[:D_MODEL])
    gb = jax.nn.sigmoid(gb + b_gate[D_MODEL:])
    return qa, ka, va, qb, kb, vb, ga, gb


def conv_ffn(h, w_up, conv_w, conv_b, w_down, prev):
    u = h @ w_up
    t = u.shape[1]
    up = jnp.concatenate([prev.astype(u.dtype), u], axis=1)
    c = conv_b + conv_w[0] * up[:, 0:t]
    for j in range(1, CONV_W):
        c = c + conv_w[j] * up[:, j:j + t]
    a, g = jnp.split(c, 2, axis=-1)
    return (jax.nn.gelu(a, approximate=False) * g) @ w_down, up[:, -(CONV_W - 1):]


def trunk_layer(x, p, cache):
    (n1, w_in, b_gate, qa_g, ka_g, qb_g, kb_g, sinks, table,
     w_out, n2, w_up, conv_w, conv_b, w_down) = p
    b, t, _ = x.shape
    h = rmsnorm(x, n1)
    qa, ka, va, qb, kb, vb, ga, gb = project(h, w_in, b_gate, qa_g, ka_g, qb_g, kb_g)
    sink = sinks.reshape(A_KV_HEADS, A_GROUP)
    if cache is None:
        span_a, span_b = (A_PREV + 1) * CHUNK, (B_PREV + 1) * CHUNK
        oa = band_prompt(qa, ka, va, A_PREV, alibi_bias(rel_dist(CHUNK, span_a, A_PREV * CHUNK)), sink)
        ob = band_prompt(qb, kb, vb, B_PREV, rel_bias(table, rel_dist(CHUNK, span_b, B_PREV * CHUNK)), None)
        conv_prev = jnp.zeros((b, CONV_W - 1, 2 * D_FF), x.dtype)
        wa, wb = min(A_WINDOW, t), min(B_REACH, t)
        new_kv = (ka[:, -wa:], va[:, -wa:], kb[:, -wb:], vb[:, -wb:])
    else:
        cak, cav, cbk, cbv, conv_prev = cache
        wa, wb = cak.shape[1], cbk.shape[1]
        oa = attend(qa, jnp.concatenate([cak.astype(ka.dtype), ka], axis=1),
                    jnp.concatenate([cav.astype(va.dtype), va], axis=1),
                    alibi_bias(rel_dist(t, wa + t, wa)), sink)
        ob = attend(qb, jnp.concatenate([cbk.astype(kb.dtype), kb], axis=1),
                    jnp.concatenate([cbv.astype(vb.dtype), vb], axis=1),
                    rel_bias(table, rel_dist(t, wb + t, wb)), None)
        new_kv = (ka, va, kb, vb)
    mixed = ga * oa.reshape(b, t, D_MODEL) + gb * ob.reshape(b, t, D_MODEL)
    x = x + mixed @ w_out
    f, conv_state = conv_ffn(rmsnorm(x, n2), w_up, conv_w, conv_b, w_down, conv_prev)
    x = x + f
    return x, (new_kv[0], new_kv[1], new_kv[2], new_kv[3], conv_state)


def setup_inputs(seed: int = 0) -> dict:
    key = jax.random.key(seed)
    ks = jax.random.split(key, 22)
    nrm = lambda k, shape, scale: scale * jax.random.normal(k, shape, jnp.float32)
    wa, wb = min(A_WINDOW, PAST_LEN), min(B_REACH, PAST_LEN)
    return {
        'x_prompt': nrm(ks[0], (BATCH, SEQ, D_MODEL), 1.0),
        'x_sample': nrm(ks[1], (DEC_BATCH, DEC_SEQ, D_MODEL), 1.0),
        'cache_a_k': nrm(ks[2], (DEPTH, DEC_BATCH, wa, A_KV_HEADS, HEAD_DIM), 1.0),
        'cache_a_v': nrm(ks[3], (DEPTH, DEC_BATCH, wa, A_KV_HEADS, HEAD_DIM), 1.0),
        'cache_b_k': nrm(ks[4], (DEPTH, DEC_BATCH, wb, B_HEADS, HEAD_DIM), 1.0),
        'cache_b_v': nrm(ks[5], (DEPTH, DEC_BATCH, wb, B_HEADS, HEAD_DIM), 1.0),
        'cache_ffn_conv': nrm(ks[6], (DEPTH, DEC_BATCH, CONV_W - 1, 2 * D_FF), 1.0),
        'norm1_g': 1.0 + nrm(ks[7], (DEPTH, D_MODEL), 0.02),
        'w_in': nrm(ks[8], (DEPTH, D_MODEL, N_IN), D_MODEL ** -0.5),
        'b_gate': nrm(ks[9], (DEPTH, 2 * D_MODEL), 0.02),
        'qn_a_g': 1.0 + nrm(ks[10], (DEPTH, HEAD_DIM), 0.02),
        'kn_a_g': 1.0 + nrm(ks[11], (DEPTH, HEAD_DIM), 0.02),
        'qn_b_g': 1.0 + nrm(ks[12], (DEPTH, HEAD_DIM), 0.02),
        'kn_b_g': 1.0 + nrm(ks[13], (DEPTH, HEAD_DIM), 0.02),
        'sinks_a': nrm(ks[14], (DEPTH, A_HEADS), 0.5),
        'rel_bias_b': nrm(ks[15], (DEPTH, N_REL, B_HEADS), 0.1),
        'w_out': nrm(ks[16], (DEPTH, D_MODEL, D_MODEL), D_MODEL ** -0.5),
        'norm2_g': 1.0 + nrm(ks[17], (DEPTH, D_MODEL), 0.02),
        'w_up': nrm(ks[18], (DEPTH, D_MODEL, 2 * D_FF), D_MODEL ** -0.5),
        'conv_w': nrm(ks[19], (DEPTH, CONV_W, 2 * D_FF), CONV_W ** -0.5),
        'conv_b': nrm(ks[20], (DEPTH, 2 * D_FF), 0.02),
        'w_down': nrm(ks[21], (DEPTH, D_FF, D_MODEL), D_FF ** -0.5),
    }


def reference(x_prompt, x_sample, cache_a_k, cache_a_v, cache_b_k, cache_b_v, cache_ffn_conv,
              norm1_g, w_in, b_gate, qn_a_g, kn_a_g, qn_b_g, kn_b_g, sinks_a, rel_bias_b,
              w_out, norm2_g, w_up, conv_w, conv_b, w_down):
    xp, xs = x_prompt, x_sample
    sp, ss = [], []
    for l in range(DEPTH):
        p = (norm1_g[l], w_in[l], b_gate[l], qn_a_g[l], kn_a_g[l], qn_b_g[l], kn_b_g[l],
             sinks_a[l], rel_bias_b[l], w_out[l], norm2_g[l], w_up[l], conv_w[l], conv_b[l], w_down[l])
        xp, st_p = trunk_layer(xp, p, None)
        sp.append(st_p)
        xs, st_s = trunk_layer(xs, p, (cache_a_k[l], cache_a_v[l], cache_b_k[l], cache_b_v[l],
                                       cache_ffn_conv[l]))
        ss.append(st_s)
    stk = lambda states, i: jnp.stack([s[i] for s in states])
    return (xp, xs,
            stk(sp, 0), stk(sp, 1), stk(sp, 2), stk(sp, 3), stk(sp, 4),
            stk(ss, 0), stk(ss, 1), stk(ss, 2), stk(ss, 3), stk(ss, 4))
```

```python
import os
import numpy as np
import concourse.bass as bass
import concourse.mybir as mybir
from concourse.bass_utils import run_bass_kernel_spmd

F32 = mybir.dt.float32
BF16 = mybir.dt.bfloat16
ALU = mybir.AluOpType
ACTF = mybir.ActivationFunctionType

ENGS = ['pe', 'act', 'dve', 'pool', 'sp']
BURST_KEYS = ('const',)


class Prog:
    def __init__(self, nc):
        self.nc = nc
        self.ops = {e: [] for e in ENGS}
        self.lastw = {}
        self.readers = {}
        self.dma_count = {}
        self.known = {e: {} for e in ENGS}
        self.known_dma = {e: {} for e in ENGS}
        self.psum_last = {}

    def add(self, eng, fn, reads=(), writes=(), dma_key=None):
        deps = set()
        rawset = set()
        for u in reads:
            t = self.lastw.get(u)
            if t is not None:
                deps.add(t)
                rawset.add(t)
        for u in writes:
            t = self.lastw.get(u)
            if t is not None:
                deps.add(t)
            r = self.readers.get(u)
            if r:
                deps.update(r.values())
        psum_units = [u for u in list(reads) + list(writes) if u.startswith('bank')]
        for u in psum_units:
            for e2, t in self.psum_last.get(u, {}).items():
                if e2 != eng:
                    deps.add(t)
        idx = len(self.ops[eng])
        if dma_key is not None:
            self.dma_count[dma_key] = self.dma_count.get(dma_key, 0) + 1
            tok = ('dma', dma_key, self.dma_count[dma_key])
        else:
            tok = ('eng', eng, idx)
        eng_need = {}
        dma_need = {}
        for d in deps:
            if d[0] == 'eng':
                _, p, i = d
                if p == eng:
                    if eng == 'pe' or dma_key is not None:
                        continue
                    eng_need[p] = max(eng_need.get(p, -1), i)
                    continue
                if i > eng_need.get(p, -1):
                    eng_need[p] = i
            else:
                _, k, c = d
                if dma_key is not None and k == dma_key and k in BURST_KEYS:
                    continue
                if c > dma_need.get(k, 0):
                    dma_need[k] = c
        waits = []
        for p, i in eng_need.items():
            if self.known[eng].get(p, -1) >= i:
                continue
            self.known[eng][p] = i
            waits.append(('eng', p, i))
        for k, c in dma_need.items():
            if self.known_dma[eng].get(k, 0) >= c:
                continue
            self.known_dma[eng][k] = c
            waits.append(('dma', k, c))
        self.ops[eng].append(dict(fn=fn, waits=waits, dma_key=dma_key, signal=False))
        for u in psum_units:
            self.psum_last.setdefault(u, {})[eng] = tok
        for u in writes:
            self.lastw[u] = tok
            self.readers[u] = {}
        for u in reads:
            r = self.readers.setdefault(u, {})
            key = (tok[0], tok[1])
            old = r.get(key)
            if old is None or old[2] < tok[2]:
                r[key] = tok
        return tok

    def emit(self, final_dma_keys=()):
        nc = self.nc
        for e in ENGS:
            for op in self.ops[e]:
                for w in op['waits']:
                    if w[0] == 'eng':
                        self.ops[w[1]][w[2]]['signal'] = True
        for e in ENGS:
            c = 0
            for op in self.ops[e]:
                if op['signal'] and op['dma_key'] is None:
                    c += 1
                    op['sigval'] = c
        sems = {e: nc.alloc_semaphore(name='s_' + e) for e in ENGS}
        dsems = {k: nc.alloc_semaphore(name='d_%s' % k) for k in self.dma_count}

        def run(e, engobj):
            for op in self.ops[e]:
                for w in op['waits']:
                    if w[0] == 'eng':
                        engobj.wait_ge(sems[w[1]], self.ops[w[1]][w[2]]['sigval'])
                    else:
                        engobj.wait_ge(dsems[w[1]], 16 * (self.dma_count[w[1]] if w[1] in BURST_KEYS else w[2]))
                ins = op['fn'](engobj)
                if op['dma_key'] is not None:
                    ins.then_inc(dsems[op['dma_key']], 16)
                elif op['signal']:
                    ins.then_inc(sems[e], 1)
            if e == 'sp':
                for k in final_dma_keys:
                    engobj.wait_ge(dsems[k], 16 * self.dma_count[k])

        with nc.Block() as block:
            @block.tensor
            def _(t):
                run('pe', t)

            @block.scalar
            def _(t):
                run('act', t)

            @block.vector
            def _(t):
                run('dve', t)

            @block.gpsimd
            def _(t):
                run('pool', t)

            @block.sync
            def _(t):
                run('sp', t)


D = 1024
NIN = 6656
DFF = 2816
NGP = 25
G_OWN = 9
GS = 28
RB = 8
RA = 5
EPS = 1e-6
MASKV = -30000.0
SLOPES = [2.0 ** (-8.0 * (h + 1) / 16.0) for h in range(16)]
SET0 = [0, 1, 2, 3, 8, 9, 10, 11]
SET1 = [4, 5, 6, 7, 12, 13, 14, 15]
TILES = [(0, 4, 'kv', None), (4, 1, 'full', 'kv'), (5, 4, 'full', 'tail'),
         (9, 4, 'full', 'full'), (13, 4, 'full', 'full'), (17, 4, 'full', 'full'), (21, 4, 'full', 'full')]


def build_program():
    nc = bass.Bass("TRN2", target_bir_lowering=False)
    P = Prog(nc)

    def din(name, shape):
        return nc.dram_tensor(name, list(shape), F32, kind="ExternalInput").ap()

    def dout(name, shape):
        return nc.dram_tensor(name, list(shape), F32, kind="ExternalOutput").ap()

    xin = din("xin", [NGP * 128, D])
    xs_in = din("xs", [128, D])
    valid_d = din("valid", [128, 32])
    validbc_d = din("validbc", [128, 640])
    cak = din("cak", [2, 128, 256]); cav = din("cav", [2, 128, 256])
    cbk = din("cbk", [2, 512, 1024]); cbv = din("cbv", [2, 512, 1024])
    cconv = din("cconv", [2, 2, 2 * DFF])
    n1_d = din("norm1_g", [2, D]); w_in = din("w_in", [2, D, NIN]); bg_d = din("b_gate", [2, 2 * D])
    qag_d = din("qn_a_g", [2, 64]); kag_d = din("kn_a_g", [2, 64]); qbg_d = din("qn_b_g", [2, 64]); kbg_d = din("kn_b_g", [2, 64])
    sinks_d = din("sinks_a", [2, 16]); rel_d = din("rel_bias_b", [2, 257, 16])
    w_out = din("w_out", [2, D, D]); n2_d = din("norm2_g", [2, D]); w_up = din("w_up", [2, D, 2 * DFF])
    cw_d = din("conv_w", [2, 3, 2 * DFF]); cb_d = din("conv_b", [2, 2 * DFF]); w_down = din("w_down", [2, DFF, D])
    ident_d = din("ident", [128, 128]); jflip_d = din("jflip", [128, 128]); ones_d = din("ones", [128, 128])
    blk_d = din("blockones", [128, 128]); alibi_d = din("alibiD", [128, 2, 128])

    y_o = dout("y", [2048, D]); ys_o = dout("ys", [128, D])
    o_ak = dout("o_ak", [2, 128, 256]); o_av = dout("o_av", [2, 128, 256])
    o_bk = dout("o_bk", [2, 512, 1024]); o_bv = dout("o_bv", [2, 512, 1024])
    o_pc = dout("o_pc", [2, 2, 2 * DFF])
    o_sak = dout("o_sak", [2, 128, 256]); o_sav = dout("o_sav", [2, 128, 256])
    o_sbk = dout("o_sbk", [2, 128, 1024]); o_sbv = dout("o_sbv", [2, 128, 1024])
    o_sc = dout("o_sc", [2, 2, 2 * DFF])
    ext_d = nc.dram_tensor("ext_scr", [2, 16, 384], F32, kind="Internal").ap()
    bias_scr = nc.dram_tensor("bias_scr", [2, 128, 16 * 2 * 128], BF16, kind="Internal").ap()

    def sb(name, shape, dt=F32):
        return nc.alloc_sbuf_tensor("sb_" + name, list(shape), dt)

    bank = [nc.alloc_psum_tensor("bank%d" % i, [128, 512], F32) for i in range(8)]

    x = sb("x", [128, 8, 512]); hT = sb("hT", [128, 8, 512], BF16)
    sqh = [sb("sqh%d" % i, [128, 512], BF16) for i in range(2)]
    rstd = sb("rstd", [128, 512]); rstdh = [sb("rstdh%d" % i, [128, 512]) for i in range(2)]
    qaT = sb("qaT", [128, 8, 512], BF16)
    cbuf = [qaT[:, 2 * i:2 * i + 2, :].rearrange("p a t -> p (a t)").bitcast(F32) for i in range(4)]
    R = sb("R", [128, 24, 512], BF16)
    kbT = [sb("kbT%d" % l, [128, 8, RB * 128], BF16) for l in range(2)]
    kaT = [sb("kaT%d" % l, [128, 4, RA * 128], BF16) for l in range(2)]
    vb = [sb("vb%d" % l, [128, RB, 16, 65], BF16) for l in range(2)]
    va = [sb("va%d" % l, [128, RA, 4, 65], BF16) for l in range(2)]
    PT = [sb("PT%d" % i, [128, 5, 128], BF16) for i in range(2)]
    tmpS = [sb("tmpS%d" % i, [128, 2, 128]) for i in range(2)]
    biasB = sb("biasB", [128, 16, 2, 128], BF16)
    alibiD = sb("alibiD", [128, 2, 128])
    stage = [sb("stage%d" % i, [128, 1024]) for i in range(2)]
    tmix = sb("tmix", [128, 8, 128], BF16)
    rec = [sb("rec%d" % i, [128, 4]) for i in range(2)]
    wbuf = [sb("wbuf%d" % i, [128, 8 * 512], BF16) for i in range(4)]
    ident = sb("ident", [128, 128]); jflip = sb("jflip", [128, 128]); identb = sb("identb", [128, 128], BF16)
    onesb = sb("onesb", [128, 128], BF16); blockb = sb("blockb", [128, 128], BF16); ctmp = sb("ctmp", [128, 128])
    valid = sb("valid", [128, 32]); validbc = sb("validbc", [128, 640])
    ones16 = sb("ones16", [128, 16]); epsc = sb("epsc", [128, 1])
    n1 = sb("n1", [128, 2, 8]); n2 = sb("n2", [128, 2, 8]); bgc = sb("bgc", [128, 2, 16])
    gq_a = sb("gq_a", [128, 2]); gk_a = sb("gk_a", [128, 2]); gq_b = sb("gq_b", [128, 2]); gk_b = sb("gk_b", [128, 2])
    esink = sb("esink", [128, 2, 16]); constB = sb("constB", [128, 2, 16]); constBm = sb("constBm", [128, 2, 16])
    convw = sb("convw", [128, 2, 3, 44]); convb = sb("convb", [128, 2, 44]); cprev = sb("cprev", [128, 2, 44, 2])
    hank = sb("hank", [128, 128])

    SLOW = dict(allow_slow_non_contiguous=True)

    def dma(q, out, in_, key, reads=(), writes=(), **kw):
        P.add(q, lambda e: e.dma_start(out=out, in_=in_, **kw), reads=reads, writes=writes, dma_key=key)

    def act(out, in_, func, reads, writes, bias=None, scale=None):
        kw = {}
        if bias is not None:
            kw['bias'] = bias
        if scale is not None:
            kw['scale'] = scale
        P.add('act', lambda e: e.activation(out=out, in_=in_, func=func, **kw), reads=reads, writes=writes)

    def mm(out, lhsT, rhs, start, stop, reads, writes):
        P.add('pe', lambda e: e.matmul(out, lhsT=lhsT, rhs=rhs, start=start, stop=stop), reads=reads, writes=writes)

    def tr(out, in_, idn, reads, writes):
        P.add('pe', lambda e: e.transpose(out, in_, idn), reads=reads, writes=writes)

    def dve(fn, reads, writes, eng='dve'):
        P.add(eng, fn, reads=reads, writes=writes)

    for name, t, d in (("ident", ident, ident_d), ("jflip", jflip, jflip_d), ("alibiD", alibiD, alibi_d),
                       ("valid", valid, valid_d), ("validbc", validbc, validbc_d)):
        dma('sp', t[:], d, 'const', writes=[name])
    for l in range(2):
        for gt, gd in ((gq_a, qag_d), (gk_a, kag_d), (gq_b, qbg_d), (gk_b, kbg_d)):
            for hf in range(2):
                dma('sp', gt[hf * 64:(hf + 1) * 64, l:l + 1], gd[l:l + 1, :].rearrange("o p -> p o"), 'const', writes=['gcols'])
        dma('sp', esink[:, l, :], sinks_d[l, :].partition_broadcast(128), 'const', writes=['esink'])
        dma('sp', constB[:, l, :], rel_d[l, 256, :].partition_broadcast(128), 'const', writes=['constB'])
    dve(lambda e: e.tensor_copy(out=identb[:], in_=ident[:]), ['ident'], ['identb'])
    dma('sp', ctmp[:], ones_d, 'ctmp', writes=['ctmp'])
    dve(lambda e: e.tensor_copy(out=onesb[:], in_=ctmp[:]), ['ctmp'], ['onesb'])
    dma('sp', ctmp[:], blk_d, 'ctmp', writes=['ctmp'])
    dve(lambda e: e.tensor_copy(out=blockb[:], in_=ctmp[:]), ['ctmp'], ['blockb'])
    dve(lambda e: e.memset(ones16[:], 1.0), [], ['ones16'])
    dve(lambda e: e.memset(epsc[:], EPS), [], ['epsc'])
    dve(lambda e: e.memset(cprev[:], 0.0), [], ['cprev0', 'cprev1'])
    extS = stage[0][0:16, 0:384]

    def load_T(src_ap, n, dst, dst_units):
        dma('sp', hank[0:n, :], src_ap, 'hank', writes=['hank'])
        b_ = cnt['pp'] % 2
        cnt['pp'] += 1
        tr(bank[b_][:, 0:n], hank[0:n, :], ident[0:n, 0:n], ['hank', 'ident'], ['bank%d' % b_])
        dve(lambda e: e.tensor_copy(out=dst, in_=bank[b_][:, 0:n]), ['bank%d' % b_], dst_units)

    cnt = dict(pp=0, nn=0, S=0, O=0, st=0, c=0)
    STOP = int(os.environ.get('KDBG_STOP', '99'))
    ATT = int(os.environ.get('KDBG_ATT', '99'))
    for l in range(2):
        load_T(n1_d[l].rearrange("(k p) -> k p", p=128), 8, n1[:, l, :], ['n1'])
        load_T(n2_d[l].rearrange("(k p) -> k p", p=128), 8, n2[:, l, :], ['n2'])
        load_T(bg_d[l].rearrange("(k p) -> k p", p=128), 16, bgc[:, l, :], ['bgc'])
        load_T(cb_d[l].rearrange("(k p) -> k p", p=128), 44, convb[:, l, :], ['convb'])
        for j in range(3):
            load_T(cw_d[l, j].rearrange("(k p) -> k p", p=128), 44, convw[:, l, j, :], ['convw'])
        for t_ in range(2):
            dma('sp', hank[:, 0:16], rel_d[l, 1 + 128 * t_:129 + 128 * t_, :], 'hank', writes=['hank'])
            b_ = cnt['pp'] % 2
            cnt['pp'] += 1
            tr(bank[b_][0:16, 0:128], hank[:, 0:16], ident[:], ['hank', 'ident'], ['bank%d' % b_])
            dve(lambda e, b_=b_, t_=t_: e.tensor_copy(out=extS[:, 128 * t_:128 * t_ + 128], in_=bank[b_][0:16, 0:128]), ['bank%d' % b_], ['stage0'])
        dve(lambda e: e.tensor_copy(out=extS[:, 256:384], in_=extS[:, 255:256].broadcast_to([16, 128])), ['stage0'], ['stage0'])
        dma('sp', ext_d[l], extS, 'ext', reads=['stage0'])
    dve(lambda e: e.tensor_scalar_mul(out=gq_a[:], in0=gq_a[:], scalar1=0.125), ['gcols'], ['gcols'])
    dve(lambda e: e.tensor_scalar_mul(out=gq_b[:], in0=gq_b[:], scalar1=0.125), ['gcols'], ['gcols'])
    act(esink[:], esink[:], ACTF.Exp, ['esink'], ['esink'])
    dve(lambda e: e.tensor_copy(out=constBm[:], in_=constB[:]), ['constB'], ['constBm'])
    dve(lambda e: e.memset(constBm[0:64, :, :], MASKV), ['constBm'], ['constBm'])
    P.lastw['extscr'] = ('dma', 'ext', P.dma_count['ext'])
    PPB0 = [0, 1, 5, 6]
    hstage = [wbuf[i][:].bitcast(F32).rearrange("p (h q) -> p h q", q=128) for i in range(2)]
    nflip = 0
    for l in range(2):
        for dl in range(2):
            hs = hstage[dl]
            src = bass.AP(tensor=ext_d.tensor, offset=l * 16 * 384 + 128 * dl, ap=[[1, 128], [384, 16], [1, 128]])
            dma('sp', hs, src, 'hst%d' % dl, reads=['extscr'], writes=['w%d' % dl])
            for h in range(16):
                bi = PPB0[(nflip // 4) % 4]
                bk = bank[bi]
                c0 = (nflip % 4) * 128
                mm(bk[:, c0:c0 + 128], jflip[:], hs[:, h, :], True, True, ['jflip', 'w%d' % dl], ['bank%d' % bi])
                dve(lambda e, h=h, dl=dl, bk=bk, l=l, c0=c0: e.tensor_scalar(out=biasB[:, h, 1 - dl, :], in0=bk[:, c0:c0 + 128], scalar1=constB[:, l, h:h + 1],
                                                                            scalar2=None, op0=ALU.subtract),
                    ['bank%d' % bi, 'constB'], ['biasB'])
                nflip += 1
        dve(lambda e: e.memset(biasB[64:128, :, 1, 0:64], MASKV), ['biasB'], ['biasB'])
        dma('sp', bias_scr[l], biasB[:].rearrange("p h d q -> p (h d q)"), 'bscr', reads=['biasB'])
        P.lastw['bias_scr%d' % l] = ('dma', 'bscr', P.dma_count['bscr'])

    wstate = dict(n=0)

    wcache = nc.dram_tensor("wcache", [68, 128, 4096], BF16, kind="Internal").ap()
    wc_idx = {}

    def wload(specs, wid):
        s = wstate['n'] % 4
        wstate['n'] += 1
        if wid in wc_idx:
            ci = wc_idx[wid]
            dma('pool', wbuf[s][:], wcache[ci], 'w%d' % s, reads=['wc%d' % ci], writes=['w%d' % s])
        else:
            ci = len(wc_idx)
            wc_idx[wid] = ci
            for (view, src) in specs:
                dma('pool', view(s), src, 'w%d' % s, writes=['w%d' % s])
            dma('sp', wcache[ci], wbuf[s][:], 'wcst%d' % s, reads=['w%d' % s], writes=['wc%d' % ci])
        return s

    def w3(s, kc, n):
        return wbuf[s][:, 0:kc * n].rearrange("p (k n) -> p k n", n=n)

    def ring_segs(g0, ng, nslots):
        segs = []
        gi = 0
        while gi < ng:
            s0 = (g0 + gi) % nslots
            n = min(ng - gi, nslots - s0)
            segs.append((gi * 128, n * 128, s0 * 128))
            gi += n
        return segs

    PPB = [0, 1, 5, 6]

    def next_pp():
        b = PPB[cnt['pp'] % 4]
        cnt['pp'] += 1
        return b

    def next_stage():
        b = cnt['st'] % 2
        cnt['st'] += 1
        return b

    def rmsnorm(l, gcols, T):
        for k in range(8):
            i = cnt['nn'] % 2
            cnt['nn'] += 1
            act(sqh[i][:, 0:T], x[:, k, 0:T], ACTF.Square, ['x%d' % k], ['sqh%d' % i])
            mm(bank[2][:, 0:T], onesb[:], sqh[i][:, 0:T], k == 0, k == 7, ['onesb', 'sqh%d' % i], ['bank2'])
        act(rstd[:, 0:T], bank[2][:, 0:T], ACTF.Ln, ['bank2', 'epsc'], ['rstd'], bias=epsc[:, 0:1], scale=1.0 / D)
        act(rstd[:, 0:T], rstd[:, 0:T], ACTF.Exp, ['rstd'], ['rstd'], scale=-0.5)
        for k in range(8):
            dve(lambda e, k=k: e.scalar_tensor_tensor(out=hT[:, k, 0:T], in0=x[:, k, 0:T], scalar=gcols[:, l, k:k + 1],
                                                      in1=rstd[:, 0:T], op0=ALU.mult, op1=ALU.mult),
                ['x%d' % k, 'rstd', 'n1', 'n2'], ['hT'])

    def proj_fm(s, col0, T, kc=8, n=512, rhs_fn=None, rhs_reads=('hT',), lo=0, dup64=False):
        b = next_pp()
        W = w3(s, kc, n)
        for k in range(kc):
            rhs = hT[:, k, lo:T] if rhs_fn is None else rhs_fn(k)
            mm(bank[b][:, lo:T], W[:, k, col0:col0 + 128], rhs, k == 0, k == kc - 1, ['w%d' % s] + list(rhs_reads), ['bank%d' % b])
        return b

    hn_deferred = []

    def hn_flush():
        while hn_deferred:
            hn_deferred.pop(0)()

    def headnorm(b, T, gcol, dsts, dst_units, lo=0, after=None):
        i = cnt['nn'] % 2
        cnt['nn'] += 1
        nb = (2, 7)[i]
        act(sqh[i][:, lo:T], bank[b][:, lo:T], ACTF.Square, ['bank%d' % b], ['sqh%d' % i])
        mm(bank[nb][:, lo:T], blockb[:], sqh[i][:, lo:T], True, True, ['blockb', 'sqh%d' % i], ['bank%d' % nb])
        hn_flush()

        def late():
            act(rstdh[i][:, lo:T], bank[nb][:, lo:T], ACTF.Ln, ['bank%d' % nb, 'epsc'], ['rstdh%d' % i], bias=epsc[:, 0:1], scale=1.0 / 64)
            act(rstdh[i][:, lo:T], rstdh[i][:, lo:T], ACTF.Exp, ['rstdh%d' % i], ['rstdh%d' % i], scale=-0.5)
            for dd in dsts:
                c0, n, dst = dd[0], dd[1], dd[2]
                p0, p1 = dd[3] if len(dd) > 3 else (0, 128)
                dve(lambda e, c0=c0, n=n, dst=dst, p0=p0, p1=p1: e.scalar_tensor_tensor(
                    out=dst, in0=bank[b][p0:p1, c0:c0 + n], scalar=gcol[p0:p1, :],
                    in1=rstdh[i][p0:p1, c0:c0 + n], op0=ALU.mult, op1=ALU.mult),
                    ['bank%d' % b, 'rstdh%d' % i, 'gcols'], dst_units)
            if after is not None:
                after()
        hn_deferred.append(late)

    def win_src(l, c0, n=512):
        return w_in[l].rearrange("(k p) n -> p k n", p=128)[:, :, c0:c0 + n]

    def kv_out_T(l, g0, ng, ring, nblk, nslots, dst_fn, unit):
        for gi in range(ng):
            slot = (g0 + gi) % nslots
            b = next_pp()
            bv = bank[b][:].bitcast(BF16)
            for j in range(nblk):
                tr(bv[:, j * 128:(j + 1) * 128], ring[:, j, slot * 128:(slot + 1) * 128], identb[:], [unit, 'identb'], ['bank%d' % b])
            st = next_stage()
            dve(lambda e, st=st, bv=bv: e.tensor_copy(out=stage[st][:, 0:nblk * 128], in_=bv[:, 0:nblk * 128]), ['bank%d' % b], ['stage%d' % st])
            dma('sp', dst_fn(gi), stage[st][:, 0:nblk * 128], 'stage%d' % st, reads=['stage%d' % st])

    pend = []
    deferred = []

    def run_deferred():
        while deferred:
            deferred.pop(0)()


    def pipe(pe_fn, post_fn, depth=1):
        r_ = pe_fn()
        while len(pend) >= depth:
            pend.pop(0)()
        pend.append(lambda: post_fn(r_))

    def flush():
        while pend:
            pend.pop(0)()
        hn_flush()

    def layer(*a, **k):
        layer_body(*a, **k)
        flush()

    def layer_body(l, g0, ng, kind, sample=False, kvout=None, tlast=None, mask_out=None, yout=None, qlo=0, skip_down=False):
        T = ng * 128
        tlast = T if tlast is None else tlast
        L = str(l)
        full = kind == 'full'
        lo = qlo
        rmsnorm(l, n1, T)
        s = wload([(lambda w: w3(w, 8, 512), win_src(l, 1024))], (l, 'kava'))
        for jp in range(2):
            def post_ka(b, jp=jp):
                segs = ring_segs(g0, ng, RA)
                dsts = []
                for (c0, n, rc) in segs:
                    dsts.append((c0, n, kaT[l][0:64, 2 * jp, rc:rc + n], (0, 64)))
                    dsts.append((c0, n, kaT[l][64:128, 2 * jp + 1, rc:rc + n], (64, 128)))
                def dup(segs=segs, jp=jp):
                    for (c0, n, rc) in segs:
                        dma('sp', kaT[l][64:128, 2 * jp, rc:rc + n], kaT[l][0:64, 2 * jp, rc:rc + n], 'kadup0', reads=['kaT' + L], writes=['kaT' + L])
                        dma('sp', kaT[l][0:64, 2 * jp + 1, rc:rc + n], kaT[l][64:128, 2 * jp + 1, rc:rc + n], 'kadup1', reads=['kaT' + L], writes=['kaT' + L])
                headnorm(b, T, gk_a[:, l:l + 1], dsts, ['kaT' + L], after=dup)
            pipe(lambda jp=jp, s=s: proj_fm(s, jp * 128, T), post_ka)
        W = w3(s, 8, 512)
        for gi in range(ng):
            g = g0 + gi

            def pe_va(gi=gi, s=s, W=W):
                b = next_pp()
                for k in range(8):
                    mm(bank[b][:, 0:256], hT[:, k, gi * 128:(gi + 1) * 128], W[:, k, 256:512], k == 0, k == 7, ['hT', 'w%d' % s], ['bank%d' % b])
                return b

            def post_va(b, g=g):
                hn_flush()
                slot = g % RA
                act(va[l][:, slot, :, 0:64], bank[b][:, 0:256].rearrange("p (h d) -> p h d", d=64), ACTF.Copy,
                    ['bank%d' % b, 'valid'], ['va' + L], scale=valid[:, g:g + 1])
                act(va[l][:, slot, :, 64], ones16[:, 0:4], ACTF.Copy, ['ones16', 'valid'], ['va' + L], scale=valid[:, g:g + 1])
                if kvout is not None and kvout['a'](g) is not None:
                    st = next_stage()
                    dve(lambda e, b=b, st=st: e.tensor_copy(out=stage[st][:, 0:256], in_=bank[b][:, 0:256]), ['bank%d' % b], ['stage%d' % st])
                    dma('sp', kvout['av'][l], stage[st][:, 0:256], 'stage%d' % st, reads=['stage%d' % st])
            pipe(pe_va, post_va)
        if kvout is not None:
            flush()
            ga_last = [gi for gi in range(ng) if kvout['a'](g0 + gi) is not None]
            for gi in ga_last:
                slot_ = (g0 + gi) % RA
                b_ = next_pp()
                bv_ = bank[b_][:].bitcast(BF16)
                for j in range(4):
                    tr(bv_[:, j * 128:(j + 1) * 128], kaT[l][:, j, slot_ * 128:(slot_ + 1) * 128], identb[:], ['kaT' + L, 'identb'], ['bank%d' % b_])
                st_ = next_stage()
                dve(lambda e, st_=st_, bv_=bv_: e.tensor_copy(out=stage[st_][:, 0:256].rearrange("p (h d) -> p h d", d=64),
                                                              in_=bv_[:, 0:512].rearrange("p (h d) -> p h d", d=128)[:, :, 0:64]), ['bank%d' % b_], ['stage%d' % st_])
                dma('sp', kvout['ak'][l], stage[st_][:, 0:256], 'stage%d' % st_, reads=['stage%d' % st_])
        for half in range(2):
            s = wload([(lambda w: w3(w, 8, 512), win_src(l, 2560 + half * 512))], (l, 'kb', half))
            for j in range(4):
                def post_kb(b, half=half, j=j):
                    dsts = [(c0, n, kbT[l][:, half * 4 + j, rc:rc + n]) for (c0, n, rc) in ring_segs(g0, ng, RB)]
                    headnorm(b, T, gk_b[:, l:l + 1], dsts, ['kbT' + L])
                pipe(lambda s=s, j=j: proj_fm(s, j * 128, T), post_kb)
        if kvout is not None:
            flush()
            kv_out_T(l, g0, ng, kbT[l], 8, RB, lambda gi: kvout['bk'](l, g0 + gi), 'kbT' + L)
        for half in range(2):
            s = wload([(lambda w: w3(w, 8, 512), win_src(l, 3584 + half * 512))], (l, 'vb', half))
            W = w3(s, 8, 512)
            for gi in range(ng):
                g = g0 + gi

                def pe_vb(gi=gi, s=s, W=W):
                    b = next_pp()
                    for k in range(8):
                        mm(bank[b][:, 0:512], hT[:, k, gi * 128:(gi + 1) * 128], W[:, k, 0:512], k == 0, k == 7, ['hT', 'w%d' % s], ['bank%d' % b])
                    return b

                def post_vb(b, g=g, half=half):
                    hn_flush()
                    slot = g % RB
                    act(vb[l][:, slot, half * 8:(half + 1) * 8, 0:64], bank[b][:, 0:512].rearrange("p (h d) -> p h d", d=64), ACTF.Copy,
                        ['bank%d' % b, 'valid'], ['vb' + L], scale=valid[:, g:g + 1])
                    if half == 0:
                        act(vb[l][:, slot, :, 64], ones16[:, 0:16], ACTF.Copy, ['ones16', 'valid'], ['vb' + L], scale=valid[:, g:g + 1])
                    if kvout is not None:
                        st = next_stage()
                        dve(lambda e, b=b, st=st: e.tensor_copy(out=stage[st][:, 0:512], in_=bank[b][:, 0:512]), ['bank%d' % b], ['stage%d' % st])
                        dma('sp', kvout['bv'](l, g)[:, half * 512:(half + 1) * 512], stage[st][:, 0:512], 'stage%d' % st, reads=['stage%d' % st])
                pipe(pe_vb, post_vb)
        if not full:
            return
        dma('sp', biasB[:].rearrange("p h d q -> p (h d q)"), bias_scr[l], 'biasB', reads=['bias_scr%d' % l], writes=['biasB'])
        for half in range(2):
            s = wload([(lambda w: w3(w, 8, 512), win_src(l, half * 512))], (l, 'qa', half))
            for jb in range(4):
                def post_qa(b, blk=half * 4 + jb):
                    headnorm(b, T, gq_a[:, l:l + 1], [(lo, T - lo, qaT[:, blk, lo:T])], ['qaT%d' % blk], lo=lo)
                pipe(lambda s=s, jb=jb: proj_fm(s, jb * 128, T, lo=lo), post_qa)
        for half in range(2):
            s = wload([(lambda w: w3(w, 8, 512), win_src(l, 1536 + half * 512))], (l, 'qb', half))
            for j in range(4):
                def post_qb(b, blk=half * 4 + j):
                    headnorm(b, T, gq_b[:, l:l + 1], [(lo, T - lo, R[:, blk, lo:T])], ['R%d' % blk], lo=lo)
                pipe(lambda s=s, j=j: proj_fm(s, j * 128, T, lo=lo), post_qb)
        for gi_, c0 in ((0, 4608), (1, 5632)):
            for half in range(2):
                s = wload([(lambda w: w3(w, 8, 512), win_src(l, c0 + half * 512))], (l, 'g', gi_, half))
                for j in range(4):
                    def post_g(b, blk=half * 4 + j, gi_=gi_):
                        hn_flush()
                        rj = 8 + gi_ * 8 + blk
                        act(R[:, rj, lo:T], bank[b][:, lo:T], ACTF.Sigmoid, ['bank%d' % b, 'bgc'], ['R%d' % rj],
                            bias=bgc[:, l, gi_ * 8 + blk: gi_ * 8 + blk + 1])
                    pipe(lambda s=s, j=j: proj_fm(s, j * 128, T, lo=lo), post_g)
        flush()
        for gi in range(lo // 128, ng):
            g = g0 + gi
            qc0 = gi * 128
            st_a = next_stage()
            st_b = next_stage()
            Oa = stage[st_a][:].bitcast(BF16)
            Ob = stage[st_b][:].bitcast(BF16)
            for typ in ('b', 'a'):
                for hc in range(4):
                    ob = 5 + cnt['O'] % 2
                    cnt['O'] += 1
                    Ov = bank[ob][:, 0:260].rearrange("p (h e) -> p h e", e=65)
                    for hh in range(4):
                        h = hc * 4 + hh
                        si = cnt['S'] % 2
                        xb_i, yb_i = ((3, 4), (7, 2), (0, 1))[cnt['S'] % 3]
                        cnt['S'] += 1
                        Xb = bank[xb_i]
                        Yb = bank[yb_i]
                        xu = 'bank%d' % xb_i
                        yu = 'bank%d' % yb_i
                        pu = 'PT%d' % si
                        tu = 'tmpS%d' % si
                        if typ == 'b':
                            def pe_s(h=h, Xb=Xb, Yb=Yb, xu=xu, yu=yu):
                                pb = 64 * (h % 2)
                                hp = h // 2
                                for d in range(5):
                                    slot = (g - 4 + d) % RB
                                    if d < 3:
                                        out, ou_ = Xb[:, d * 128:(d + 1) * 128], xu
                                    else:
                                        out, ou_ = Yb[:, (d - 3) * 128:(d - 2) * 128], yu
                                    mm(out, kbT[l][pb:pb + 64, hp, slot * 128:(slot + 1) * 128], R[pb:pb + 64, hp, qc0:qc0 + 128],
                                       True, True, ['kbT' + L, 'R%d' % hp], [ou_])

                            def post_s(_, h=h, hh=hh, hc=hc, si=si, Xb=Xb, Yb=Yb, xu=xu, yu=yu, pu=pu, tu=tu, Ov=Ov, ob=ob):
                                act(PT[si][:, 0:3, :], Xb[:, 0:384].rearrange("p (a b) -> p a b", b=128), ACTF.Exp, [xu], [pu])
                                dve(lambda e: e.memset(PT[si][0:64, 0, 64:128], 0.0), [pu], [pu], eng='pool')
                                dve(lambda e: e.tensor_tensor(out=tmpS[si][:, 0:2, :], in0=Yb[:, 0:256].rearrange("p (a b) -> p a b", b=128),
                                                              in1=biasB[:, h, 0:2, :], op=ALU.add), [yu, 'biasB'], [tu])
                                run_deferred()

                                def late():
                                    act(PT[si][:, 3:5, :], tmpS[si][:, 0:2, :], ACTF.Exp, [tu], [pu])
                                    for d in range(5):
                                        slot = (g - 4 + d) % RB
                                        mm(Ov[:, hh, :], PT[si][:, d, :], vb[l][:, slot, h, :], d == 0, d == 4, [pu, 'vb' + L], ['bank%d' % ob])
                                    if hh == 3:
                                        norm_o('b', hc, Ov, ob, Ob, 'stage%d' % st_b, l)
                                deferred.append(late)
                        else:
                            def pe_s(h=h, Yb=Yb, yu=yu):
                                kvh = h // 4
                                pb = 64 * (h % 2)
                                bq = h // 2
                                for d in range(2):
                                    slot = (g - 1 + d) % RA
                                    mm(Yb[:, d * 128:(d + 1) * 128], kaT[l][pb:pb + 64, kvh, slot * 128:(slot + 1) * 128],
                                       qaT[pb:pb + 64, bq, qc0:qc0 + 128], True, True, ['kaT' + L, 'qaT%d' % bq], [yu])

                            def post_s(_, h=h, hh=hh, hc=hc, si=si, Yb=Yb, yu=yu, pu=pu, tu=tu, Ov=Ov, ob=ob):
                                kvh = h // 4
                                dve(lambda e: e.scalar_tensor_tensor(
                                    out=tmpS[si][:, 0:2, :], in0=alibiD[:, 0:2, :], scalar=-SLOPES[h],
                                    in1=Yb[:, 0:256].rearrange("p (a b) -> p a b", b=128), op0=ALU.mult, op1=ALU.add),
                                    [yu, 'alibiD'], [tu])
                                run_deferred()

                                def late():
                                    act(PT[si][:, 0:2, :], tmpS[si][:, 0:2, :], ACTF.Exp, [tu], [pu])
                                    for d in range(2):
                                        slot = (g - 1 + d) % RA
                                        mm(Ov[:, hh, :], PT[si][:, d, :], va[l][:, slot, kvh, :], d == 0, d == 1, [pu, 'va' + L], ['bank%d' % ob])
                                    if hh == 3:
                                        norm_o('a', hc, Ov, ob, Oa, 'stage%d' % st_a, l)
                                deferred.append(late)
                        pipe(pe_s, post_s, depth=2)
            flush()
            run_deferred()
            Ta = bank[0][:].bitcast(BF16)
            Tb = bank[1][:].bitcast(BF16)
            for j in range(8):
                tr(Ta[:, j * 128:(j + 1) * 128], Oa[:, j * 128:(j + 1) * 128], identb[:], ['stage%d' % st_a, 'identb'], ['bank0'])
            for j in range(8):
                tr(Tb[:, j * 128:(j + 1) * 128], Ob[:, j * 128:(j + 1) * 128], identb[:], ['stage%d' % st_b, 'identb'], ['bank1'])
            gunits = ['R%d' % r for r in range(8, 24)]
            dve(lambda e, qc0=qc0, Ta=Ta: e.tensor_tensor(out=hT[:, :, qc0:qc0 + 128], in0=Ta[:, 0:1024].rearrange("p (j t) -> p j t", t=128),
                                                          in1=R[:, 8:16, qc0:qc0 + 128], op=ALU.mult), ['bank0'] + gunits, ['hT'])
            dve(lambda e, qc0=qc0, Tb=Tb: e.tensor_tensor(out=tmix[:], in0=Tb[:, 0:1024].rearrange("p (j t) -> p j t", t=128),
                                                          in1=R[:, 16:24, qc0:qc0 + 128], op=ALU.mult), ['bank1'] + gunits, ['tmix'])
            dve(lambda e, qc0=qc0: e.tensor_tensor(out=hT[:, :, qc0:qc0 + 128], in0=hT[:, :, qc0:qc0 + 128], in1=tmix[:], op=ALU.add),
                ['hT', 'tmix'], ['hT'], eng='pool')
        wo = w_out[l].rearrange("(k p) n -> p k n", p=128)
        for half in range(2):
            s = wload([(lambda w: w3(w, 8, 512), wo[:, :, half * 512:(half + 1) * 512])], (l, 'wo', half))
            for j in range(4):
                def post_wo(b, blk=half * 4 + j):
                    dve(lambda e: e.tensor_tensor(out=x[:, blk, lo:T], in0=x[:, blk, lo:T], in1=bank[b][:, lo:T], op=ALU.add),
                        ['bank%d' % b, 'x%d' % blk], ['x%d' % blk])
                pipe(lambda s=s, j=j: proj_fm(s, j * 128, T, lo=lo), post_wo)
        flush()
        rmsnorm(l, n2, T)
        wu = w_up[l].rearrange("(k p) n -> p k n", p=128)
        cp = cprev[:, l, :, :]
        cpu = 'cprev' + L
        for p in range(11):
            s = wload([(lambda w: w3(w, 8, 512)[:, :, 0:256], wu[:, :, 256 * p:256 * p + 256]),
                       (lambda w: w3(w, 8, 512)[:, :, 256:512], wu[:, :, DFF + 256 * p:DFF + 256 * p + 256])], (l, 'wu', p))
            for jj in range(2):
                ja = 2 * p + jj
                pair = {}
                for which, blk, col0 in (('a', ja, jj * 128), ('g', 22 + ja, 256 + jj * 128)):
                    def post_c(b, which=which, blk=blk, ja=ja, pair=pair):
                        ci = cnt['c'] % 4
                        cnt['c'] += 1
                        c = cbuf[ci]
                        cu = 'qaT%d' % (2 * ci)
                        cu2 = 'qaT%d' % (2 * ci + 1)
                        bu = 'bank%d' % b
                        ps = bank[b]
                        act(c[:, lo:T], ps[:, lo:T], ACTF.Identity, [bu, 'convw', 'convb'], [cu, cu2],
                            scale=convw[:, l, 2, blk:blk + 1], bias=convb[:, l, blk:blk + 1])
                        dve(lambda e: e.scalar_tensor_tensor(out=c[:, lo + 1:T], in0=ps[:, lo:T - 1], scalar=convw[:, l, 1, blk:blk + 1],
                                                             in1=c[:, lo + 1:T], op0=ALU.mult, op1=ALU.add), [bu, cu, 'convw'], [cu])
                        dve(lambda e: e.scalar_tensor_tensor(out=c[:, lo + 2:T], in0=ps[:, lo:T - 2], scalar=convw[:, l, 0, blk:blk + 1],
                                                             in1=c[:, lo + 2:T], op0=ALU.mult, op1=ALU.add), [bu, cu, 'convw'], [cu])
                        dve(lambda e: e.scalar_tensor_tensor(out=c[:, lo:lo + 2], in0=cp[:, blk, 0:2], scalar=convw[:, l, 0, blk:blk + 1],
                                                             in1=c[:, lo:lo + 2], op0=ALU.mult, op1=ALU.add), [cpu, cu, 'convw'], [cu])
                        dve(lambda e: e.scalar_tensor_tensor(out=c[:, lo:lo + 1], in0=cp[:, blk, 1:2], scalar=convw[:, l, 1, blk:blk + 1],
                                                             in1=c[:, lo:lo + 1], op0=ALU.mult, op1=ALU.add), [cpu, cu, 'convw'], [cu])
                        dve(lambda e: e.tensor_copy(out=cp[:, blk, :], in_=ps[:, tlast - 2:tlast]), [bu, cpu], [cpu])
                        pair[which] = (c, cu)
                        if which == 'g':
                            run_deferred()

                            def late():
                                (ca, cau), (cg, cgu) = pair['a'], pair['g']
                                act(ca[:, lo:T], ca[:, lo:T], ACTF.Gelu, [cau], [cau])
                                dve(lambda e: e.tensor_tensor(out=R[:, ja, lo:T], in0=ca[:, lo:T], in1=cg[:, lo:T], op=ALU.mult),
                                    [cau, cgu], ['R%d' % ja])
                            deferred.append(late)
                    pipe(lambda s=s, col0=col0: proj_fm(s, col0, T, lo=lo), post_c)
        flush()
        run_deferred()
        if skip_down:
            return
        wd = w_down[l].rearrange("(k p) n -> p k n", p=128)
        for j in range(8):
            s = wload([(lambda w: w3(w, 22, 128), wd[:, :, j * 128:(j + 1) * 128])], (l, 'wd', j))

            def post_wd(b, j=j):
                dve(lambda e: e.tensor_tensor(out=x[:, j, 0:T], in0=x[:, j, 0:T], in1=bank[b][:, 0:T], op=ALU.add),
                    ['bank%d' % b, 'x%d' % j], ['x%d' % j])
            pipe(lambda s=s: proj_fm(s, 0, T, kc=22, n=128, rhs_fn=lambda k: R[:, k, 0:T], rhs_reads=['R%d' % r for r in range(22)]), post_wd)
        flush()
        if mask_out is not None:
            dve(lambda e: e.tensor_tensor(out=x[:, :, 0:T], in0=x[:, :, 0:T], in1=validbc[:, mask_out:mask_out + T].unsqueeze(1).broadcast_to([128, 8, T]), op=ALU.mult),
                ['x%d' % k for k in range(8)] + ['validbc'], ['x%d' % k for k in range(8)])
        if yout is not None:
            for gi in range(ng):
                st = next_stage()
                for hb in range(2):
                    b = next_pp()
                    for jj in range(4):
                        j = hb * 4 + jj
                        tr(bank[b][:, jj * 128:(jj + 1) * 128], x[:, j, gi * 128:(gi + 1) * 128], ident[:], ['x%d' % j, 'ident'], ['bank%d' % b])
                    dve(lambda e, b=b, st=st, hb=hb: e.tensor_copy(out=stage[st][:, hb * 512:(hb + 1) * 512], in_=bank[b][:, 0:512]),
                        ['bank%d' % b], ['stage%d' % st])
                dma('sp', yout(gi), stage[st][:], 'stage%d' % st, reads=['stage%d' % st])

    def norm_o(typ, hc, Ov, ob, Odst, ou, l):
        ri = ob - 5
        ru = 'rec%d' % ri
        if typ == 'b':
            dve(lambda e: e.tensor_scalar_max(out=rec[ri][:], in0=Ov[:, :, 64], scalar1=1e-30), ['bank%d' % ob], [ru])
        else:
            dve(lambda e: e.tensor_tensor(out=rec[ri][:], in0=Ov[:, :, 64], in1=esink[:, l, hc * 4:(hc + 1) * 4], op=ALU.add),
                ['bank%d' % ob, 'esink'], [ru])
        dve(lambda e: e.reciprocal(out=rec[ri][:], in_=rec[ri][:]), [ru], [ru])
        dve(lambda e: e.tensor_tensor(
            out=Odst[:, hc * 256:(hc + 1) * 256].rearrange("p (h d) -> p h d", d=64), in0=Ov[:, :, 0:64],
            in1=rec[ri][:, 0:4].unsqueeze(2).broadcast_to([128, 4, 64]), op=ALU.mult),
            ['bank%d' % ob, ru], [ou])

    def load_x(src_fn, ng):
        for gi in range(ng):
            st = next_stage()
            dma('sp', stage[st][:], src_fn(gi), 'stage%d' % st, writes=['stage%d' % st])
            for hb in range(2):
                b = next_pp()
                for jj in range(4):
                    j = hb * 4 + jj
                    tr(bank[b][:, jj * 128:(jj + 1) * 128], stage[st][:, j * 128:(j + 1) * 128], ident[:], ['stage%d' % st, 'ident'], ['bank%d' % b])
                dve(lambda e, b=b, hb=hb, gi=gi: e.tensor_copy(out=x[:, hb * 4:(hb + 1) * 4, gi * 128:(gi + 1) * 128],
                                                              in_=bank[b][:, 0:512].rearrange("p (j t) -> p j t", t=128)),
                    ['bank%d' % b], ['x%d' % j for j in range(hb * 4, hb * 4 + 4)])

    def conv_out(l, dst, key):
        for r_ in range(2):
            b_ = next_pp()
            tr(bank[b_][0:44, 0:128], cprev[:, l, :, r_], ident[:], ['cprev%d' % l, 'ident'], ['bank%d' % b_])
            st = next_stage()
            dve(lambda e, b_=b_, st=st: e.tensor_copy(out=stage[st][0:44, 0:128], in_=bank[b_][0:44, 0:128]), ['bank%d' % b_], ['stage%d' % st])
            dma('sp', dst[l, r_].rearrange("(k p) -> k p", p=128), stage[st][0:44, 0:128], 'stage%d' % st, reads=['stage%d' % st])

    NT_DBG = int(os.environ.get('KDBG_NT', '99'))
    L1_DBG = os.environ.get('KDBG_L1')
    for (g0, ng, k1, k2) in TILES[:NT_DBG]:
        if L1_DBG is not None:
            k1, k2 = L1_DBG, None
        load_x(lambda gi, g0=g0: xin[(g0 + gi) * 128:(g0 + gi + 1) * 128, :], ng)
        last = (g0 == 21)
        kvo = None
        if last:
            kvo = dict(a=lambda g: (True if g == 24 else None), ak=[o_ak[0], o_ak[1]], av=[o_av[0], o_av[1]],
                       bk=lambda l, g: o_bk[l, (g - 21) * 128:(g - 20) * 128, :], bv=lambda l, g: o_bv[l, (g - 21) * 128:(g - 20) * 128, :])
        layer(0, g0, ng, k1, kvout=kvo, mask_out={4: 0, 5: 128}.get(g0))
        if k2 == 'tail':
            layer(1, g0, ng, 'full', qlo=(ng - 1) * 128, skip_down=True)
        elif k2 is not None:
            layer(1, g0, ng, k2, kvout=kvo,
                  yout=(lambda gi, g0=g0: y_o[(g0 - G_OWN + gi) * 128:(g0 - G_OWN + gi + 1) * 128, :]) if k2 == 'full' else None)
    for l in range(2):
        conv_out(l, o_pc, 'pcout')

    for l in (range(2) if NT_DBG > 50 else ()):
        L = str(l)
        for r_ in range(2):
            load_T(cconv[l, r_].rearrange("(k p) -> k p", p=128), 44, cprev[:, l, :, r_], ['cprev' + L])
        for cg in range(4):
            slot = (GS - 4 + cg) % RB
            st = next_stage()
            dma('sp', stage[st][:], cbk[l, cg * 128:(cg + 1) * 128, :], 'stage%d' % st, writes=['stage%d' % st])
            for hb in range(2):
                b = next_pp()
                for jj in range(4):
                    j = hb * 4 + jj
                    tr(bank[b][:, jj * 128:(jj + 1) * 128], stage[st][:, j * 128:(j + 1) * 128], ident[:], ['stage%d' % st, 'ident'], ['bank%d' % b])
                dve(lambda e, b=b, hb=hb, slot=slot, l=l: e.tensor_copy(out=kbT[l][:, hb * 4:(hb + 1) * 4, slot * 128:(slot + 1) * 128],
                                                                       in_=bank[b][:, 0:512].rearrange("p (j t) -> p j t", t=128)),
                    ['bank%d' % b], ['kbT' + L])
            st = next_stage()
            dma('sp', stage[st][:], cbv[l, cg * 128:(cg + 1) * 128, :], 'stage%d' % st, writes=['stage%d' % st])
            dve(lambda e, st=st, slot=slot, l=l: e.tensor_copy(out=vb[l][:, slot, :, 0:64], in_=stage[st][:].rearrange("p (h d) -> p h d", d=64)),
                ['stage%d' % st], ['vb' + L])
            dve(lambda e, slot=slot, l=l: e.tensor_copy(out=vb[l][:, slot, :, 64], in_=ones16[:, 0:16]), ['ones16'], ['vb' + L])
        slot = (GS - 1) % RA
        st = next_stage()
        dma('sp', stage[st][:, 0:256], cak[l], 'stage%d' % st, writes=['stage%d' % st])
        b = next_pp()
        for j in range(2):
            tr(bank[b][:, j * 128:(j + 1) * 128], stage[st][:, j * 128:(j + 1) * 128], ident[:], ['stage%d' % st, 'ident'], ['bank%d' % b])
        for j in range(2):
            sc_ = slice(slot * 128, (slot + 1) * 128)
            dve(lambda e, b=b, j=j, sc_=sc_, l=l: e.tensor_copy(out=kaT[l][0:64, 2 * j, sc_], in_=bank[b][0:64, j * 128:(j + 1) * 128]), ['bank%d' % b], ['kaT' + L])
            dve(lambda e, b=b, j=j, sc_=sc_, l=l: e.tensor_copy(out=kaT[l][64:128, 2 * j + 1, sc_], in_=bank[b][64:128, j * 128:(j + 1) * 128]), ['bank%d' % b], ['kaT' + L])
            dma('sp', kaT[l][64:128, 2 * j, sc_], kaT[l][0:64, 2 * j, sc_], 'kadup0', reads=['kaT' + L], writes=['kaT' + L])
            dma('sp', kaT[l][0:64, 2 * j + 1, sc_], kaT[l][64:128, 2 * j + 1, sc_], 'kadup1', reads=['kaT' + L], writes=['kaT' + L])
        st = next_stage()
        dma('sp', stage[st][:, 0:256], cav[l], 'stage%d' % st, writes=['stage%d' % st])
        dve(lambda e, st=st, slot=slot, l=l: e.tensor_copy(out=va[l][:, slot, :, 0:64], in_=stage[st][:, 0:256].rearrange("p (h d) -> p h d", d=64)),
            ['stage%d' % st], ['va' + L])
        dve(lambda e, slot=slot, l=l: e.tensor_copy(out=va[l][:, slot, :, 64], in_=ones16[:, 0:4]), ['ones16'], ['va' + L])
    if NT_DBG <= 50:
        P.emit(final_dma_keys=[k for k in ('stage0', 'stage1') if k in P.dma_count])
        return nc
    load_x(lambda gi: xs_in, 1)
    kvo = dict(a=lambda g: True, ak=[o_sak[0], o_sak[1]], av=[o_sav[0], o_sav[1]],
               bk=lambda l, g: o_sbk[l], bv=lambda l, g: o_sbv[l])
    layer(0, GS, 1, 'full', kvout=kvo, tlast=32)
    layer(1, GS, 1, 'full', kvout=kvo, tlast=32, yout=lambda gi: ys_o)
    for l in range(2):
        conv_out(l, o_sc, 'scout')

    P.emit(final_dma_keys=['stage0', 'stage1'])
    return nc


_CACHE = {}


def _consts():
    ident = np.eye(128, dtype=np.float32)
    jflip = np.ascontiguousarray(ident[::-1])
    ones = np.ones((128, 128), np.float32)
    blk = np.zeros((128, 128), np.float32)
    blk[:64, :64] = 1.0
    blk[64:, 64:] = 1.0
    k = np.arange(128)[:, None]
    q = np.arange(128)[None, :]
    alibi = np.zeros((128, 2, 128), np.float32)
    BIG = 3.0e7
    d0 = np.abs(128 + q - k).astype(np.float32)
    d0 = np.where((k < 64) & (q >= 64), BIG, d0)
    d1 = np.abs(q - k).astype(np.float32)
    d1 = np.where((k >= 64) & (q < 64), BIG, d1)
    alibi[:, 0, :] = d0
    alibi[:, 1, :] = d1
    return dict(ident=ident, jflip=jflip, ones=ones, blockones=blk, alibiD=alibi)


def kernel(x_prompt, x_sample, cache_a_k, cache_a_v, cache_b_k, cache_b_v, cache_ffn_conv,
           norm1_g, w_in, b_gate, qn_a_g, kn_a_g, qn_b_g, kn_b_g, sinks_a, rel_bias_b,
           w_out, norm2_g, w_up, conv_w, conv_b, w_down):
    f = lambda a: np.ascontiguousarray(np.asarray(a, dtype=np.float32))
    x_prompt = f(x_prompt); x_sample = f(x_sample)
    if 'nc' not in _CACHE:
        _CACHE['nc'] = build_program()
    nc = _CACHE['nc']
    consts = _consts()
    shared = dict(norm1_g=f(norm1_g), w_in=f(w_in), b_gate=f(b_gate), qn_a_g=f(qn_a_g), kn_a_g=f(kn_a_g),
                  qn_b_g=f(qn_b_g), kn_b_g=f(kn_b_g), sinks_a=f(sinks_a), rel_bias_b=f(rel_bias_b), w_out=f(w_out),
                  norm2_g=f(norm2_g), w_up=f(w_up), conv_w=f(conv_w), conv_b=f(conv_b), w_down=f(w_down), **consts)
    HALO = 18 * 64
    in_maps = []
    for c in range(8):
        b, half = c // 2, c % 2
        start = half * 2048
        xin = np.zeros((NGP * 128, D), np.float32)
        lo = start - HALO
        s0 = max(lo, 0)
        xin[s0 - lo:, :] = x_prompt[b, s0:start + 2048, :]
        pos = lo + np.arange(NGP * 128)
        vmask = (pos >= 0).astype(np.float32)
        valid = np.zeros((128, 32), np.float32)
        valid[:, :NGP] = vmask.reshape(NGP, 128).T
        valid[:32, GS] = 1.0
        validbc = np.ascontiguousarray(np.broadcast_to(vmask[4 * 128:9 * 128][None, :], (128, 640))).astype(np.float32)
        xs = np.zeros((128, D), np.float32)
        xs[:32] = x_sample[c]
        m = dict(shared)
        m.update(xin=xin, xs=xs, valid=valid, validbc=validbc,
                 cak=f(np.asarray(cache_a_k)[:, c].reshape(2, 128, 256)), cav=f(np.asarray(cache_a_v)[:, c].reshape(2, 128, 256)),
                 cbk=f(np.asarray(cache_b_k)[:, c].reshape(2, 512, 1024)), cbv=f(np.asarray(cache_b_v)[:, c].reshape(2, 512, 1024)),
                 cconv=f(np.asarray(cache_ffn_conv)[:, c]))
        in_maps.append(m)
    res = run_bass_kernel_spmd(nc, in_maps, core_ids=list(range(8)))
    r = res.results
    y_prompt = np.stack([np.concatenate([r[2 * b]["y"], r[2 * b + 1]["y"]], axis=0) for b in range(4)])
    y_sample = np.stack([r[c]["ys"][:32] for c in range(8)])
    odd = [1, 3, 5, 7]
    pak = np.stack([r[c]["o_ak"] for c in odd], axis=1).reshape(2, 4, 128, 4, 64)
    pav = np.stack([r[c]["o_av"] for c in odd], axis=1).reshape(2, 4, 128, 4, 64)
    pbk = np.stack([r[c]["o_bk"] for c in odd], axis=1).reshape(2, 4, 512, 16, 64)
    pbv = np.stack([r[c]["o_bv"] for c in odd], axis=1).reshape(2, 4, 512, 16, 64)
    pcv = np.stack([r[c]["o_pc"] for c in odd], axis=1)
    sak = np.stack([r[c]["o_sak"][:, :32] for c in range(8)], axis=1).reshape(2, 8, 32, 4, 64)
    sav = np.stack([r[c]["o_sav"][:, :32] for c in range(8)], axis=1).reshape(2, 8, 32, 4, 64)
    sbk = np.stack([r[c]["o_sbk"][:, :32] for c in range(8)], axis=1).reshape(2, 8, 32, 16, 64)
    sbv = np.stack([r[c]["o_sbv"][:, :32] for c in range(8)], axis=1).reshape(2, 8, 32, 16, 64)
    scv = np.stack([r[c]["o_sc"] for c in range(8)], axis=1)
    outs = (y_prompt, y_sample, pak, pav, pbk, pbv, pcv, sak, sav, sbk, sbv, scv)
    return tuple(np.ascontiguousarray(o, dtype=np.float32) for o in outs)
```

```python
import os
import numpy as np
import concourse.bass as bass
import concourse.mybir as mybir
from concourse.bass_utils import run_bass_kernel_spmd

F32 = mybir.dt.float32
BF16 = mybir.dt.bfloat16
ALU = mybir.AluOpType
ACTF = mybir.ActivationFunctionType

ENGS = ['pe', 'act', 'dve', 'pool', 'sp']
BURST_KEYS = ('const',)


class Prog:
    def __init__(self, nc):
        self.nc = nc
        self.ops = {e: [] for e in ENGS}
        self.lastw = {}
        self.readers = {}
        self.dma_count = {}
        self.known = {e: {} for e in ENGS}
        self.known_dma = {e: {} for e in ENGS}
        self.psum_last = {}

    def add(self, eng, fn, reads=(), writes=(), dma_key=None):
        deps = set()
        rawset = set()
        for u in reads:
            t = self.lastw.get(u)
            if t is not None:
                deps.add(t)
                rawset.add(t)
        for u in writes:
            t = self.lastw.get(u)
            if t is not None:
                deps.add(t)
            r = self.readers.get(u)
            if r:
                deps.update(r.values())
        psum_units = [u for u in list(reads) + list(writes) if u.startswith('bank')]
        for u in psum_units:
            for e2, t in self.psum_last.get(u, {}).items():
                if e2 != eng:
                    deps.add(t)
        idx = len(self.ops[eng])
        if dma_key is not None:
            self.dma_count[dma_key] = self.dma_count.get(dma_key, 0) + 1
            tok = ('dma', dma_key, self.dma_count[dma_key])
        else:
            tok = ('eng', eng, idx)
        eng_need = {}
        dma_need = {}
        for d in deps:
            if d[0] == 'eng':
                _, p, i = d
                if p == eng:
                    if eng == 'pe' or dma_key is not None:
                        continue
                    eng_need[p] = max(eng_need.get(p, -1), i)
                    continue
                if i > eng_need.get(p, -1):
                    eng_need[p] = i
            else:
                _, k, c = d
                if dma_key is not None and k == dma_key and k in BURST_KEYS:
                    continue
                if c > dma_need.get(k, 0):
                    dma_need[k] = c
        waits = []
        for p, i in eng_need.items():
            if self.known[eng].get(p, -1) >= i:
                continue
            self.known[eng][p] = i
            waits.append(('eng', p, i))
        for k, c in dma_need.items():
            if self.known_dma[eng].get(k, 0) >= c:
                continue
            self.known_dma[eng][k] = c
            waits.append(('dma', k, c))
        self.ops[eng].append(dict(fn=fn, waits=waits, dma_key=dma_key, signal=False))
        for u in psum_units:
            self.psum_last.setdefault(u, {})[eng] = tok
        for u in writes:
            self.lastw[u] = tok
            self.readers[u] = {}
        for u in reads:
            r = self.readers.setdefault(u, {})
            key = (tok[0], tok[1])
            old = r.get(key)
            if old is None or old[2] < tok[2]:
                r[key] = tok
        return tok

    def emit(self, final_dma_keys=()):
        nc = self.nc
        for e in ENGS:
            for op in self.ops[e]:
                for w in op['waits']:
                    if w[0] == 'eng':
                        self.ops[w[1]][w[2]]['signal'] = True
        for e in ENGS:
            c = 0
            for op in self.ops[e]:
                if op['signal'] and op['dma_key'] is None:
                    c += 1
                    op['sigval'] = c
        sems = {e: nc.alloc_semaphore(name='s_' + e) for e in ENGS}
        dsems = {k: nc.alloc_semaphore(name='d_%s' % k) for k in self.dma_count}

        def run(e, engobj):
            for op in self.ops[e]:
                for w in op['waits']:
                    if w[0] == 'eng':
                        engobj.wait_ge(sems[w[1]], self.ops[w[1]][w[2]]['sigval'])
                    else:
                        engobj.wait_ge(dsems[w[1]], 16 * (self.dma_count[w[1]] if w[1] in BURST_KEYS else w[2]))
                ins = op['fn'](engobj)
                if op['dma_key'] is not None:
                    ins.then_inc(dsems[op['dma_key']], 16)
                elif op['signal']:
                    ins.then_inc(sems[e], 1)
            if e == 'sp':
                for k in final_dma_keys:
                    engobj.wait_ge(dsems[k], 16 * self.dma_count[k])

        with nc.Block() as block:
            @block.tensor
            def _(t):
                run('pe', t)

            @block.scalar
            def _(t):
                run('act', t)

            @block.vector
            def _(t):
                run('dve', t)

            @block.gpsimd
            def _(t):
                run('pool', t)

            @block.sync
            def _(t):
                run('sp', t)


D = 1024
NIN = 6656
DFF = 2816
NGP = 25
G_OWN = 9
GS = 28
RB = 8
RA = 5
EPS = 1e-6
MASKV = -30000.0
SLOPES = [2.0 ** (-8.0 * (h + 1) / 16.0) for h in range(16)]
SET0 = [0, 1, 2, 3, 8, 9, 10, 11]
SET1 = [4, 5, 6, 7, 12, 13, 14, 15]
TILES = [(0, 4, 'kv', None), (4, 1, 'full', 'kv'), (5, 4, 'full', 'tail'),
         (9, 4, 'full', 'full'), (13, 4, 'full', 'full'), (17, 4, 'full', 'full'), (21, 4, 'full', 'full')]


def build_program():
    nc = bass.Bass("TRN2", target_bir_lowering=False)
    P = Prog(nc)

    def din(name, shape):
        return nc.dram_tensor(name, list(shape), F32, kind="ExternalInput").ap()

    def dout(name, shape):
        return nc.dram_tensor(name, list(shape), F32, kind="ExternalOutput").ap()

    xin = din("xin", [NGP * 128, D])
    xs_in = din("xs", [128, D])
    valid_d = din("valid", [128, 32])
    validbc_d = din("validbc", [128, 640])
    cak = din("cak", [2, 128, 256]); cav = din("cav", [2, 128, 256])
    cbk = din("cbk", [2, 512, 1024]); cbv = din("cbv", [2, 512, 1024])
    cconv = din("cconv", [2, 2, 2 * DFF])
    n1_d = din("norm1_g", [2, D]); w_in = din("w_in", [2, D, NIN]); bg_d = din("b_gate", [2, 2 * D])
    qag_d = din("qn_a_g", [2, 64]); kag_d = din("kn_a_g", [2, 64]); qbg_d = din("qn_b_g", [2, 64]); kbg_d = din("kn_b_g", [2, 64])
    sinks_d = din("sinks_a", [2, 16]); rel_d = din("rel_bias_b", [2, 257, 16])
    w_out = din("w_out", [2, D, D]); n2_d = din("norm2_g", [2, D]); w_up = din("w_up", [2, D, 2 * DFF])
    cw_d = din("conv_w", [2, 3, 2 * DFF]); cb_d = din("conv_b", [2, 2 * DFF]); w_down = din("w_down", [2, DFF, D])
    ident_d = din("ident", [128, 128]); jflip_d = din("jflip", [128, 128]); ones_d = din("ones", [128, 128])
    blk_d = din("blockones", [128, 128]); alibi_d = din("alibiD", [128, 2, 128])

    y_o = dout("y", [2048, D]); ys_o = dout("ys", [128, D])
    o_ak = dout("o_ak", [2, 128, 256]); o_av = dout("o_av", [2, 128, 256])
    o_bk = dout("o_bk", [2, 512, 1024]); o_bv = dout("o_bv", [2, 512, 1024])
    o_pc = dout("o_pc", [2, 2, 2 * DFF])
    o_sak = dout("o_sak", [2, 128, 256]); o_sav = dout("o_sav", [2, 128, 256])
    o_sbk = dout("o_sbk", [2, 128, 1024]); o_sbv = dout("o_sbv", [2, 128, 1024])
    o_sc = dout("o_sc", [2, 2, 2 * DFF])
    ext_d = nc.dram_tensor("ext_scr", [2, 16, 384], F32, kind="Internal").ap()
    bias_scr = nc.dram_tensor("bias_scr", [2, 128, 16 * 2 * 128], BF16, kind="Internal").ap()

    def sb(name, shape, dt=F32):
        return nc.alloc_sbuf_tensor("sb_" + name, list(shape), dt)

    bank = [nc.alloc_psum_tensor("bank%d" % i, [128, 512], F32) for i in range(8)]

    x = sb("x", [128, 8, 512]); hT = sb("hT", [128, 8, 512], BF16)
    sqh = [sb("sqh%d" % i, [128, 512], BF16) for i in range(2)]
    rstd = sb("rstd", [128, 512]); rstdh = [sb("rstdh%d" % i, [128, 512]) for i in range(2)]
    qaT = sb("qaT", [128, 8, 512], BF16)
    cbuf = [qaT[:, 2 * i:2 * i + 2, :].rearrange("p a t -> p (a t)").bitcast(F32) for i in range(4)]
    R = sb("R", [128, 24, 512], BF16)
    kbT = [sb("kbT%d" % l, [128, 8, RB * 128], BF16) for l in range(2)]
    kaT = [sb("kaT%d" % l, [128, 4, RA * 128], BF16) for l in range(2)]
    vb = [sb("vb%d" % l, [128, RB, 16, 65], BF16) for l in range(2)]
    va = [sb("va%d" % l, [128, RA, 4, 65], BF16) for l in range(2)]
    PT = [sb("PT%d" % i, [128, 5, 128], BF16) for i in range(2)]
    tmpS = [sb("tmpS%d" % i, [128, 2, 128]) for i in range(2)]
    biasB = sb("biasB", [128, 16, 2, 128], BF16)
    alibiD = sb("alibiD", [128, 2, 128])
    stage = [sb("stage%d" % i, [128, 1024]) for i in range(2)]
    tmix = sb("tmix", [128, 8, 128], BF16)
    rec = [sb("rec%d" % i, [128, 4]) for i in range(2)]
    wbuf = [sb("wbuf%d" % i, [128, 8 * 512], BF16) for i in range(4)]
    ident = sb("ident", [128, 128]); jflip = sb("jflip", [128, 128]); identb = sb("identb", [128, 128], BF16)
    onesb = sb("onesb", [128, 128], BF16); blockb = sb("blockb", [128, 128], BF16); ctmp = sb("ctmp", [128, 128])
    valid = sb("valid", [128, 32]); validbc = sb("validbc", [128, 640])
    ones16 = sb("ones16", [128, 16]); epsc = sb("epsc", [128, 1])
    n1 = sb("n1", [128, 2, 8]); n2 = sb("n2", [128, 2, 8]); bgc = sb("bgc", [128, 2, 16])
    gq_a = sb("gq_a", [128, 2]); gk_a = sb("gk_a", [128, 2]); gq_b = sb("gq_b", [128, 2]); gk_b = sb("gk_b", [128, 2])
    esink = sb("esink", [128, 2, 16]); constB = sb("constB", [128, 2, 16]); constBm = sb("constBm", [128, 2, 16])
    convw = sb("convw", [128, 2, 3, 44]); convb = sb("convb", [128, 2, 44]); cprev = sb("cprev", [128, 2, 44, 2])
    hank = sb("hank", [128, 128])

    SLOW = dict(allow_slow_non_contiguous=True)

    def dma(q, out, in_, key, reads=(), writes=(), **kw):
        P.add(q, lambda e: e.dma_start(out=out, in_=in_, **kw), reads=reads, writes=writes, dma_key=key)

    def act(out, in_, func, reads, writes, bias=None, scale=None):
        kw = {}
        if bias is not None:
            kw['bias'] = bias
        if scale is not None:
            kw['scale'] = scale
        P.add('act', lambda e: e.activation(out=out, in_=in_, func=func, **kw), reads=reads, writes=writes)

    def mm(out, lhsT, rhs, start, stop, reads, writes):
        P.add('pe', lambda e: e.matmul(out, lhsT=lhsT, rhs=rhs, start=start, stop=stop), reads=reads, writes=writes)

    def tr(out, in_, idn, reads, writes):
        P.add('pe', lambda e: e.transpose(out, in_, idn), reads=reads, writes=writes)

    def dve(fn, reads, writes, eng='dve'):
        P.add(eng, fn, reads=reads, writes=writes)

    for name, t, d in (("ident", ident, ident_d), ("jflip", jflip, jflip_d), ("alibiD", alibiD, alibi_d),
                       ("valid", valid, valid_d), ("validbc", validbc, validbc_d)):
        dma('sp', t[:], d, 'const', writes=[name])
    for l in range(2):
        for gt, gd in ((gq_a, qag_d), (gk_a, kag_d), (gq_b, qbg_d), (gk_b, kbg_d)):
            for hf in range(2):
                dma('sp', gt[hf * 64:(hf + 1) * 64, l:l + 1], gd[l:l + 1, :].rearrange("o p -> p o"), 'const', writes=['gcols'])
        dma('sp', esink[:, l, :], sinks_d[l, :].partition_broadcast(128), 'const', writes=['esink'])
        dma('sp', constB[:, l, :], rel_d[l, 256, :].partition_broadcast(128), 'const', writes=['constB'])
    dve(lambda e: e.tensor_copy(out=identb[:], in_=ident[:]), ['ident'], ['identb'])
    dma('sp', ctmp[:], ones_d, 'ctmp', writes=['ctmp'])
    dve(lambda e: e.tensor_copy(out=onesb[:], in_=ctmp[:]), ['ctmp'], ['onesb'])
    dma('sp', ctmp[:], blk_d, 'ctmp', writes=['ctmp'])
    dve(lambda e: e.tensor_copy(out=blockb[:], in_=ctmp[:]), ['ctmp'], ['blockb'])
    dve(lambda e: e.memset(ones16[:], 1.0), [], ['ones16'])
    dve(lambda e: e.memset(epsc[:], EPS), [], ['epsc'])
    dve(lambda e: e.memset(cprev[:], 0.0), [], ['cprev0', 'cprev1'])
    extS = stage[0][0:16, 0:384]

    def load_T(src_ap, n, dst, dst_units):
        dma('sp', hank[0:n, :], src_ap, 'hank', writes=['hank'])
        b_ = cnt['pp'] % 2
        cnt['pp'] += 1
        tr(bank[b_][:, 0:n], hank[0:n, :], ident[0:n, 0:n], ['hank', 'ident'], ['bank%d' % b_])
        dve(lambda e: e.tensor_copy(out=dst, in_=bank[b_][:, 0:n]), ['bank%d' % b_], dst_units)

    cnt = dict(pp=0, nn=0, S=0, O=0, st=0, c=0)
    STOP = int(os.environ.get('KDBG_STOP', '99'))
    ATT = int(os.environ.get('KDBG_ATT', '99'))
    for l in range(2):
        load_T(n1_d[l].rearrange("(k p) -> k p", p=128), 8, n1[:, l, :], ['n1'])
        load_T(n2_d[l].rearrange("(k p) -> k p", p=128), 8, n2[:, l, :], ['n2'])
        load_T(bg_d[l].rearrange("(k p) -> k p", p=128), 16, bgc[:, l, :], ['bgc'])
        load_T(cb_d[l].rearrange("(k p) -> k p", p=128), 44, convb[:, l, :], ['convb'])
        for j in range(3):
            load_T(cw_d[l, j].rearrange("(k p) -> k p", p=128), 44, convw[:, l, j, :], ['convw'])
        for t_ in range(2):
            dma('sp', hank[:, 0:16], rel_d[l, 1 + 128 * t_:129 + 128 * t_, :], 'hank', writes=['hank'])
            b_ = cnt['pp'] % 2
            cnt['pp'] += 1
            tr(bank[b_][0:16, 0:128], hank[:, 0:16], ident[:], ['hank', 'ident'], ['bank%d' % b_])
            dve(lambda e, b_=b_, t_=t_: e.tensor_copy(out=extS[:, 128 * t_:128 * t_ + 128], in_=bank[b_][0:16, 0:128]), ['bank%d' % b_], ['stage0'])
        dve(lambda e: e.tensor_copy(out=extS[:, 256:384], in_=extS[:, 255:256].broadcast_to([16, 128])), ['stage0'], ['stage0'])
        dma('sp', ext_d[l], extS, 'ext', reads=['stage0'])
    dve(lambda e: e.tensor_scalar_mul(out=gq_a[:], in0=gq_a[:], scalar1=0.125), ['gcols'], ['gcols'])
    dve(lambda e: e.tensor_scalar_mul(out=gq_b[:], in0=gq_b[:], scalar1=0.125), ['gcols'], ['gcols'])
    act(esink[:], esink[:], ACTF.Exp, ['esink'], ['esink'])
    dve(lambda e: e.tensor_copy(out=constBm[:], in_=constB[:]), ['constB'], ['constBm'])
    dve(lambda e: e.memset(constBm[0:64, :, :], MASKV), ['constBm'], ['constBm'])
    P.lastw['extscr'] = ('dma', 'ext', P.dma_count['ext'])
    PPB0 = [0, 1, 5, 6]
    hstage = [wbuf[i][:].bitcast(F32).rearrange("p (h q) -> p h q", q=128) for i in range(2)]
    nflip = 0
    for l in range(2):
        for dl in range(2):
            hs = hstage[dl]
            src = bass.AP(tensor=ext_d.tensor, offset=l * 16 * 384 + 128 * dl, ap=[[1, 128], [384, 16], [1, 128]])
            dma('sp', hs, src, 'hst%d' % dl, reads=['extscr'], writes=['w%d' % dl])
            for h in range(16):
                bi = PPB0[(nflip // 4) % 4]
                bk = bank[bi]
                c0 = (nflip % 4) * 128
                mm(bk[:, c0:c0 + 128], jflip[:], hs[:, h, :], True, True, ['jflip', 'w%d' % dl], ['bank%d' % bi])
                dve(lambda e, h=h, dl=dl, bk=bk, l=l, c0=c0: e.tensor_scalar(out=biasB[:, h, 1 - dl, :], in0=bk[:, c0:c0 + 128], scalar1=constB[:, l, h:h + 1],
                                                                            scalar2=None, op0=ALU.subtract),
                    ['bank%d' % bi, 'constB'], ['biasB'])
                nflip += 1
        dve(lambda e: e.memset(biasB[64:128, :, 1, 0:64], MASKV), ['biasB'], ['biasB'])
        dma('sp', bias_scr[l], biasB[:].rearrange("p h d q -> p (h d q)"), 'bscr', reads=['biasB'])
        P.lastw['bias_scr%d' % l] = ('dma', 'bscr', P.dma_count['bscr'])

    wstate = dict(n=0)

    wcache = nc.dram_tensor("wcache", [68, 128, 4096], BF16, kind="Internal").ap()
    wc_idx = {}

    def wload(specs, wid):
        s = wstate['n'] % 4
        wstate['n'] += 1
        if wid in wc_idx:
            ci = wc_idx[wid]
            dma('pool', wbuf[s][:], wcache[ci], 'w%d' % s, reads=['wc%d' % ci], writes=['w%d' % s])
        else:
            ci = len(wc_idx)
            wc_idx[wid] = ci
            for (view, src) in specs:
                dma('pool', view(s), src, 'w%d' % s, writes=['w%d' % s])
            dma('sp', wcache[ci], wbuf[s][:], 'wcst%d' % s, reads=['w%d' % s], writes=['wc%d' % ci])
        return s

    def w3(s, kc, n):
        return wbuf[s][:, 0:kc * n].rearrange("p (k n) -> p k n", n=n)

    def ring_segs(g0, ng, nslots):
        segs = []
        gi = 0
        while gi < ng:
            s0 = (g0 + gi) % nslots
            n = min(ng - gi, nslots - s0)
            segs.append((gi * 128, n * 128, s0 * 128))
            gi += n
        return segs

    PPB = [0, 1, 5, 6]

    def next_pp():
        b = PPB[cnt['pp'] % 4]
        cnt['pp'] += 1
        return b

    def next_stage():
        b = cnt['st'] % 2
        cnt['st'] += 1
        return b

    def rmsnorm(l, gcols, T):
        for k in range(8):
            i = cnt['nn'] % 2
            cnt['nn'] += 1
            act(sqh[i][:, 0:T], x[:, k, 0:T], ACTF.Square, ['x%d' % k], ['sqh%d' % i])
            mm(bank[2][:, 0:T], onesb[:], sqh[i][:, 0:T], k == 0, k == 7, ['onesb', 'sqh%d' % i], ['bank2'])
        act(rstd[:, 0:T], bank[2][:, 0:T], ACTF.Ln, ['bank2', 'epsc'], ['rstd'], bias=epsc[:, 0:1], scale=1.0 / D)
        act(rstd[:, 0:T], rstd[:, 0:T], ACTF.Exp, ['rstd'], ['rstd'], scale=-0.5)
        for k in range(8):
            dve(lambda e, k=k: e.scalar_tensor_tensor(out=hT[:, k, 0:T], in0=x[:, k, 0:T], scalar=gcols[:, l, k:k + 1],
                                                      in1=rstd[:, 0:T], op0=ALU.mult, op1=ALU.mult),
                ['x%d' % k, 'rstd', 'n1', 'n2'], ['hT'])

    def proj_fm(s, col0, T, kc=8, n=512, rhs_fn=None, rhs_reads=('hT',), lo=0, dup64=False):
        b = next_pp()
        W = w3(s, kc, n)
        for k in range(kc):
            rhs = hT[:, k, lo:T] if rhs_fn is None else rhs_fn(k)
            mm(bank[b][:, lo:T], W[:, k, col0:col0 + 128], rhs, k == 0, k == kc - 1, ['w%d' % s] + list(rhs_reads), ['bank%d' % b])
        return b

    def headnorm(b, T, gcol, dsts, dst_units, lo=0):
        i = cnt['nn'] % 2
        cnt['nn'] += 1
        act(sqh[i][:, lo:T], bank[b][:, lo:T], ACTF.Square, ['bank%d' % b], ['sqh%d' % i])
        nb = (2, 7)[i]
        mm(bank[nb][:, lo:T], blockb[:], sqh[i][:, lo:T], True, True, ['blockb', 'sqh%d' % i], ['bank%d' % nb])
        act(rstdh[i][:, lo:T], bank[nb][:, lo:T], ACTF.Ln, ['bank%d' % nb, 'epsc'], ['rstdh%d' % i], bias=epsc[:, 0:1], scale=1.0 / 64)
        act(rstdh[i][:, lo:T], rstdh[i][:, lo:T], ACTF.Exp, ['rstdh%d' % i], ['rstdh%d' % i], scale=-0.5)
        for dd in dsts:
            c0, n, dst = dd[0], dd[1], dd[2]
            p0, p1 = dd[3] if len(dd) > 3 else (0, 128)
            dve(lambda e, c0=c0, n=n, dst=dst, p0=p0, p1=p1: e.scalar_tensor_tensor(
                out=dst, in0=bank[b][p0:p1, c0:c0 + n], scalar=gcol[p0:p1, :],
                in1=rstdh[i][p0:p1, c0:c0 + n], op0=ALU.mult, op1=ALU.mult),
                ['bank%d' % b, 'rstdh%d' % i, 'gcols'], dst_units)

    def win_src(l, c0, n=512):
        return w_in[l].rearrange("(k p) n -> p k n", p=128)[:, :, c0:c0 + n]

    def kv_out_T(l, g0, ng, ring, nblk, nslots, dst_fn, unit):
        for gi in range(ng):
            slot = (g0 + gi) % nslots
            b = next_pp()
            bv = bank[b][:].bitcast(BF16)
            for j in range(nblk):
                tr(bv[:, j * 128:(j + 1) * 128], ring[:, j, slot * 128:(slot + 1) * 128], identb[:], [unit, 'identb'], ['bank%d' % b])
            st = next_stage()
            dve(lambda e, st=st, bv=bv: e.tensor_copy(out=stage[st][:, 0:nblk * 128], in_=bv[:, 0:nblk * 128]), ['bank%d' % b], ['stage%d' % st])
            dma('sp', dst_fn(gi), stage[st][:, 0:nblk * 128], 'stage%d' % st, reads=['stage%d' % st])

    pend = []
    deferred = []

    def run_deferred():
        while deferred:
            deferred.pop(0)()


    def pipe(pe_fn, post_fn, depth=1):
        r_ = pe_fn()
        while len(pend) >= depth:
            pend.pop(0)()
        pend.append(lambda: post_fn(r_))

    def flush():
        while pend:
            pend.pop(0)()

    def layer(*a, **k):
        layer_body(*a, **k)
        flush()

    def layer_body(l, g0, ng, kind, sample=False, kvout=None, tlast=None, mask_out=None, yout=None, qlo=0, skip_down=False):
        T = ng * 128
        tlast = T if tlast is None else tlast
        L = str(l)
        full = kind == 'full'
        lo = qlo
        rmsnorm(l, n1, T)
        s = wload([(lambda w: w3(w, 8, 512), win_src(l, 1024))], (l, 'kava'))
        for jp in range(2):
            def post_ka(b, jp=jp):
                segs = ring_segs(g0, ng, RA)
                dsts = []
                for (c0, n, rc) in segs:
                    dsts.append((c0, n, kaT[l][0:64, 2 * jp, rc:rc + n], (0, 64)))
                    dsts.append((c0, n, kaT[l][64:128, 2 * jp + 1, rc:rc + n], (64, 128)))
                headnorm(b, T, gk_a[:, l:l + 1], dsts, ['kaT' + L])
                for (c0, n, rc) in segs:
                    dma('sp', kaT[l][64:128, 2 * jp, rc:rc + n], kaT[l][0:64, 2 * jp, rc:rc + n], 'kadup0', reads=['kaT' + L], writes=['kaT' + L])
                    dma('sp', kaT[l][0:64, 2 * jp + 1, rc:rc + n], kaT[l][64:128, 2 * jp + 1, rc:rc + n], 'kadup1', reads=['kaT' + L], writes=['kaT' + L])
            pipe(lambda jp=jp, s=s: proj_fm(s, jp * 128, T), post_ka)
        W = w3(s, 8, 512)
        for gi in range(ng):
            g = g0 + gi

            def pe_va(gi=gi, s=s, W=W):
                b = next_pp()
                for k in range(8):
                    mm(bank[b][:, 0:256], hT[:, k, gi * 128:(gi + 1) * 128], W[:, k, 256:512], k == 0, k == 7, ['hT', 'w%d' % s], ['bank%d' % b])
                return b

            def post_va(b, g=g):
                slot = g % RA
                act(va[l][:, slot, :, 0:64], bank[b][:, 0:256].rearrange("p (h d) -> p h d", d=64), ACTF.Copy,
                    ['bank%d' % b, 'valid'], ['va' + L], scale=valid[:, g:g + 1])
                act(va[l][:, slot, :, 64], ones16[:, 0:4], ACTF.Copy, ['ones16', 'valid'], ['va' + L], scale=valid[:, g:g + 1])
                if kvout is not None and kvout['a'](g) is not None:
                    st = next_stage()
                    dve(lambda e, b=b, st=st: e.tensor_copy(out=stage[st][:, 0:256], in_=bank[b][:, 0:256]), ['bank%d' % b], ['stage%d' % st])
                    dma('sp', kvout['av'][l], stage[st][:, 0:256], 'stage%d' % st, reads=['stage%d' % st])
            pipe(pe_va, post_va)
        if kvout is not None:
            flush()
            ga_last = [gi for gi in range(ng) if kvout['a'](g0 + gi) is not None]
            for gi in ga_last:
                slot_ = (g0 + gi) % RA
                b_ = next_pp()
                bv_ = bank[b_][:].bitcast(BF16)
                for j in range(4):
                    tr(bv_[:, j * 128:(j + 1) * 128], kaT[l][:, j, slot_ * 128:(slot_ + 1) * 128], identb[:], ['kaT' + L, 'identb'], ['bank%d' % b_])
                st_ = next_stage()
                dve(lambda e, st_=st_, bv_=bv_: e.tensor_copy(out=stage[st_][:, 0:256].rearrange("p (h d) -> p h d", d=64),
                                                              in_=bv_[:, 0:512].rearrange("p (h d) -> p h d", d=128)[:, :, 0:64]), ['bank%d' % b_], ['stage%d' % st_])
                dma('sp', kvout['ak'][l], stage[st_][:, 0:256], 'stage%d' % st_, reads=['stage%d' % st_])
        for half in range(2):
            s = wload([(lambda w: w3(w, 8, 512), win_src(l, 2560 + half * 512))], (l, 'kb', half))
            for j in range(4):
                def post_kb(b, half=half, j=j):
                    dsts = [(c0, n, kbT[l][:, half * 4 + j, rc:rc + n]) for (c0, n, rc) in ring_segs(g0, ng, RB)]
                    headnorm(b, T, gk_b[:, l:l + 1], dsts, ['kbT' + L])
                pipe(lambda s=s, j=j: proj_fm(s, j * 128, T), post_kb)
        if kvout is not None:
            flush()
            kv_out_T(l, g0, ng, kbT[l], 8, RB, lambda gi: kvout['bk'](l, g0 + gi), 'kbT' + L)
        for half in range(2):
            s = wload([(lambda w: w3(w, 8, 512), win_src(l, 3584 + half * 512))], (l, 'vb', half))
            W = w3(s, 8, 512)
            for gi in range(ng):
                g = g0 + gi

                def pe_vb(gi=gi, s=s, W=W):
                    b = next_pp()
                    for k in range(8):
                        mm(bank[b][:, 0:512], hT[:, k, gi * 128:(gi + 1) * 128], W[:, k, 0:512], k == 0, k == 7, ['hT', 'w%d' % s], ['bank%d' % b])
                    return b

                def post_vb(b, g=g, half=half):
                    slot = g % RB
                    act(vb[l][:, slot, half * 8:(half + 1) * 8, 0:64], bank[b][:, 0:512].rearrange("p (h d) -> p h d", d=64), ACTF.Copy,
                        ['bank%d' % b, 'valid'], ['vb' + L], scale=valid[:, g:g + 1])
                    if half == 0:
                        act(vb[l][:, slot, :, 64], ones16[:, 0:16], ACTF.Copy, ['ones16', 'valid'], ['vb' + L], scale=valid[:, g:g + 1])
                    if kvout is not None:
                        st = next_stage()
                        dve(lambda e, b=b, st=st: e.tensor_copy(out=stage[st][:, 0:512], in_=bank[b][:, 0:512]), ['bank%d' % b], ['stage%d' % st])
                        dma('sp', kvout['bv'](l, g)[:, half * 512:(half + 1) * 512], stage[st][:, 0:512], 'stage%d' % st, reads=['stage%d' % st])
                pipe(pe_vb, post_vb)
        if not full:
            return
        dma('sp', biasB[:].rearrange("p h d q -> p (h d q)"), bias_scr[l], 'biasB', reads=['bias_scr%d' % l], writes=['biasB'])
        for half in range(2):
            s = wload([(lambda w: w3(w, 8, 512), win_src(l, half * 512))], (l, 'qa', half))
            for jb in range(4):
                def post_qa(b, blk=half * 4 + jb):
                    headnorm(b, T, gq_a[:, l:l + 1], [(lo, T - lo, qaT[:, blk, lo:T])], ['qaT%d' % blk], lo=lo)
                pipe(lambda s=s, jb=jb: proj_fm(s, jb * 128, T, lo=lo), post_qa)
        for half in range(2):
            s = wload([(lambda w: w3(w, 8, 512), win_src(l, 1536 + half * 512))], (l, 'qb', half))
            for j in range(4):
                def post_qb(b, blk=half * 4 + j):
                    headnorm(b, T, gq_b[:, l:l + 1], [(lo, T - lo, R[:, blk, lo:T])], ['R%d' % blk], lo=lo)
                pipe(lambda s=s, j=j: proj_fm(s, j * 128, T, lo=lo), post_qb)
        for gi_, c0 in ((0, 4608), (1, 5632)):
            for half in range(2):
                s = wload([(lambda w: w3(w, 8, 512), win_src(l, c0 + half * 512))], (l, 'g', gi_, half))
                for j in range(4):
                    def post_g(b, blk=half * 4 + j, gi_=gi_):
                        rj = 8 + gi_ * 8 + blk
                        act(R[:, rj, lo:T], bank[b][:, lo:T], ACTF.Sigmoid, ['bank%d' % b, 'bgc'], ['R%d' % rj],
                            bias=bgc[:, l, gi_ * 8 + blk: gi_ * 8 + blk + 1])
                    pipe(lambda s=s, j=j: proj_fm(s, j * 128, T, lo=lo), post_g)
        flush()
        for gi in range(lo // 128, ng):
            g = g0 + gi
            qc0 = gi * 128
            st_a = next_stage()
            st_b = next_stage()
            Oa = stage[st_a][:].bitcast(BF16)
            Ob = stage[st_b][:].bitcast(BF16)
            for typ in ('b', 'a'):
                for hc in range(4):
                    ob = 5 + cnt['O'] % 2
                    cnt['O'] += 1
                    Ov = bank[ob][:, 0:260].rearrange("p (h e) -> p h e", e=65)
                    for hh in range(4):
                        h = hc * 4 + hh
                        si = cnt['S'] % 2
                        xb_i, yb_i = ((3, 4), (7, 2), (0, 1))[cnt['S'] % 3]
                        cnt['S'] += 1
                        Xb = bank[xb_i]
                        Yb = bank[yb_i]
                        xu = 'bank%d' % xb_i
                        yu = 'bank%d' % yb_i
                        pu = 'PT%d' % si
                        tu = 'tmpS%d' % si
                        if typ == 'b':
                            def pe_s(h=h, Xb=Xb, Yb=Yb, xu=xu, yu=yu):
                                pb = 64 * (h % 2)
                                hp = h // 2
                                for d in range(5):
                                    slot = (g - 4 + d) % RB
                                    if d < 3:
                                        out, ou_ = Xb[:, d * 128:(d + 1) * 128], xu
                                    else:
                                        out, ou_ = Yb[:, (d - 3) * 128:(d - 2) * 128], yu
                                    mm(out, kbT[l][pb:pb + 64, hp, slot * 128:(slot + 1) * 128], R[pb:pb + 64, hp, qc0:qc0 + 128],
                                       True, True, ['kbT' + L, 'R%d' % hp], [ou_])

                            def post_s(_, h=h, hh=hh, hc=hc, si=si, Xb=Xb, Yb=Yb, xu=xu, yu=yu, pu=pu, tu=tu, Ov=Ov, ob=ob):
                                act(PT[si][:, 0:3, :], Xb[:, 0:384].rearrange("p (a b) -> p a b", b=128), ACTF.Exp, [xu], [pu])
                                dve(lambda e: e.memset(PT[si][0:64, 0, 64:128], 0.0), [pu], [pu], eng='pool')
                                dve(lambda e: e.tensor_tensor(out=tmpS[si][:, 0:2, :], in0=Yb[:, 0:256].rearrange("p (a b) -> p a b", b=128),
                                                              in1=biasB[:, h, 0:2, :], op=ALU.add), [yu, 'biasB'], [tu])
                                run_deferred()

                                def late():
                                    act(PT[si][:, 3:5, :], tmpS[si][:, 0:2, :], ACTF.Exp, [tu], [pu])
                                    for d in range(5):
                                        slot = (g - 4 + d) % RB
                                        mm(Ov[:, hh, :], PT[si][:, d, :], vb[l][:, slot, h, :], d == 0, d == 4, [pu, 'vb' + L], ['bank%d' % ob])
                                    if hh == 3:
                                        norm_o('b', hc, Ov, ob, Ob, 'stage%d' % st_b, l)
                                deferred.append(late)
                        else:
                            def pe_s(h=h, Yb=Yb, yu=yu):
                                kvh = h // 4
                                pb = 64 * (h % 2)
                                bq = h // 2
                                for d in range(2):
                                    slot = (g - 1 + d) % RA
                                    mm(Yb[:, d * 128:(d + 1) * 128], kaT[l][pb:pb + 64, kvh, slot * 128:(slot + 1) * 128],
                                       qaT[pb:pb + 64, bq, qc0:qc0 + 128], True, True, ['kaT' + L, 'qaT%d' % bq], [yu])

                            def post_s(_, h=h, hh=hh, hc=hc, si=si, Yb=Yb, yu=yu, pu=pu, tu=tu, Ov=Ov, ob=ob):
                                kvh = h // 4
                                dve(lambda e: e.scalar_tensor_tensor(
                                    out=tmpS[si][:, 0:2, :], in0=alibiD[:, 0:2, :], scalar=-SLOPES[h],
                                    in1=Yb[:, 0:256].rearrange("p (a b) -> p a b", b=128), op0=ALU.mult, op1=ALU.add),
                                    [yu, 'alibiD'], [tu])
                                run_deferred()

                                def late():
                                    act(PT[si][:, 0:2, :], tmpS[si][:, 0:2, :], ACTF.Exp, [tu], [pu])
                                    for d in range(2):
                                        slot = (g - 1 + d) % RA
                                        mm(Ov[:, hh, :], PT[si][:, d, :], va[l][:, slot, kvh, :], d == 0, d == 1, [pu, 'va' + L], ['bank%d' % ob])
                                    if hh == 3:
                                        norm_o('a', hc, Ov, ob, Oa, 'stage%d' % st_a, l)
                                deferred.append(late)
                        pipe(pe_s, post_s, depth=2)
            flush()
            run_deferred()
            Ta = bank[0][:].bitcast(BF16)
            Tb = bank[1][:].bitcast(BF16)
            for j in range(8):
                tr(Ta[:, j * 128:(j + 1) * 128], Oa[:, j * 128:(j + 1) * 128], identb[:], ['stage%d' % st_a, 'identb'], ['bank0'])
            for j in range(8):
                tr(Tb[:, j * 128:(j + 1) * 128], Ob[:, j * 128:(j + 1) * 128], identb[:], ['stage%d' % st_b, 'identb'], ['bank1'])
            gunits = ['R%d' % r for r in range(8, 24)]
            dve(lambda e, qc0=qc0, Ta=Ta: e.tensor_tensor(out=hT[:, :, qc0:qc0 + 128], in0=Ta[:, 0:1024].rearrange("p (j t) -> p j t", t=128),
                                                          in1=R[:, 8:16, qc0:qc0 + 128], op=ALU.mult), ['bank0'] + gunits, ['hT'])
            dve(lambda e, qc0=qc0, Tb=Tb: e.tensor_tensor(out=tmix[:], in0=Tb[:, 0:1024].rearrange("p (j t) -> p j t", t=128),
                                                          in1=R[:, 16:24, qc0:qc0 + 128], op=ALU.mult), ['bank1'] + gunits, ['tmix'])
            dve(lambda e, qc0=qc0: e.tensor_tensor(out=hT[:, :, qc0:qc0 + 128], in0=hT[:, :, qc0:qc0 + 128], in1=tmix[:], op=ALU.add),
                ['hT', 'tmix'], ['hT'], eng='pool')
        wo = w_out[l].rearrange("(k p) n -> p k n", p=128)
        for half in range(2):
            s = wload([(lambda w: w3(w, 8, 512), wo[:, :, half * 512:(half + 1) * 512])], (l, 'wo', half))
            for j in range(4):
                def post_wo(b, blk=half * 4 + j):
                    dve(lambda e: e.tensor_tensor(out=x[:, blk, lo:T], in0=x[:, blk, lo:T], in1=bank[b][:, lo:T], op=ALU.add),
                        ['bank%d' % b, 'x%d' % blk], ['x%d' % blk])
                pipe(lambda s=s, j=j: proj_fm(s, j * 128, T, lo=lo), post_wo)
        flush()
        rmsnorm(l, n2, T)
        wu = w_up[l].rearrange("(k p) n -> p k n", p=128)
        cp = cprev[:, l, :, :]
        cpu = 'cprev' + L
        for p in range(11):
            s = wload([(lambda w: w3(w, 8, 512)[:, :, 0:256], wu[:, :, 256 * p:256 * p + 256]),
                       (lambda w: w3(w, 8, 512)[:, :, 256:512], wu[:, :, DFF + 256 * p:DFF + 256 * p + 256])], (l, 'wu', p))
            for jj in range(2):
                ja = 2 * p + jj

                def pe_pair(s=s, jj=jj):
                    return (proj_fm(s, jj * 128, T, lo=lo), proj_fm(s, 256 + jj * 128, T, lo=lo))

                def post_pair(bb, ja=ja):
                    blks = (ja, 22 + ja)
                    cs = []
                    for _ in range(2):
                        ci = cnt['c'] % 4
                        cnt['c'] += 1
                        cs.append((cbuf[ci], 'qaT%d' % (2 * ci), 'qaT%d' % (2 * ci + 1)))
                    for i_ in range(2):
                        c, cu, cu2 = cs[i_]
                        act(c[:, lo:T], bank[bb[i_]][:, lo:T], ACTF.Identity, ['bank%d' % bb[i_], 'convw', 'convb'], [cu, cu2],
                            scale=convw[:, l, 2, blks[i_]:blks[i_] + 1], bias=convb[:, l, blks[i_]:blks[i_] + 1])
                    for i_ in range(2):
                        c, cu, cu2 = cs[i_]
                        dve(lambda e, c=c, i_=i_: e.scalar_tensor_tensor(out=c[:, lo + 1:T], in0=bank[bb[i_]][:, lo:T - 1], scalar=convw[:, l, 1, blks[i_]:blks[i_] + 1],
                                                                        in1=c[:, lo + 1:T], op0=ALU.mult, op1=ALU.add), ['bank%d' % bb[i_], cu, 'convw'], [cu])
                    for i_ in range(2):
                        c, cu, cu2 = cs[i_]
                        dve(lambda e, c=c, i_=i_: e.scalar_tensor_tensor(out=c[:, lo + 2:T], in0=bank[bb[i_]][:, lo:T - 2], scalar=convw[:, l, 0, blks[i_]:blks[i_] + 1],
                                                                        in1=c[:, lo + 2:T], op0=ALU.mult, op1=ALU.add), ['bank%d' % bb[i_], cu, 'convw'], [cu])
                    for i_ in range(2):
                        c, cu, cu2 = cs[i_]
                        dve(lambda e, c=c, i_=i_: e.scalar_tensor_tensor(out=c[:, lo:lo + 2], in0=cp[:, blks[i_], 0:2], scalar=convw[:, l, 0, blks[i_]:blks[i_] + 1],
                                                                        in1=c[:, lo:lo + 2], op0=ALU.mult, op1=ALU.add), [cpu, cu, 'convw'], [cu])
                    for i_ in range(2):
                        c, cu, cu2 = cs[i_]
                        dve(lambda e, c=c, i_=i_: e.scalar_tensor_tensor(out=c[:, lo:lo + 1], in0=cp[:, blks[i_], 1:2], scalar=convw[:, l, 1, blks[i_]:blks[i_] + 1],
                                                                        in1=c[:, lo:lo + 1], op0=ALU.mult, op1=ALU.add), [cpu, cu, 'convw'], [cu])
                    for i_ in range(2):
                        dve(lambda e, i_=i_: e.tensor_copy(out=cp[:, blks[i_], :], in_=bank[bb[i_]][:, tlast - 2:tlast]), ['bank%d' % bb[i_], cpu], [cpu])
                    (ca, cau, _), (cg, cgu, _) = cs
                    act(ca[:, lo:T], ca[:, lo:T], ACTF.Gelu, [cau], [cau])
                    dve(lambda e: e.tensor_tensor(out=R[:, ja, lo:T], in0=ca[:, lo:T], in1=cg[:, lo:T], op=ALU.mult), [cau, cgu], ['R%d' % ja])
                pipe(pe_pair, post_pair)
        flush()
        if skip_down:
            return
        wd = w_down[l].rearrange("(k p) n -> p k n", p=128)
        for j in range(8):
            s = wload([(lambda w: w3(w, 22, 128), wd[:, :, j * 128:(j + 1) * 128])], (l, 'wd', j))

            def post_wd(b, j=j):
                dve(lambda e: e.tensor_tensor(out=x[:, j, 0:T], in0=x[:, j, 0:T], in1=bank[b][:, 0:T], op=ALU.add),
                    ['bank%d' % b, 'x%d' % j], ['x%d' % j])
            pipe(lambda s=s: proj_fm(s, 0, T, kc=22, n=128, rhs_fn=lambda k: R[:, k, 0:T], rhs_reads=['R%d' % r for r in range(22)]), post_wd)
        flush()
        if mask_out is not None:
            dve(lambda e: e.tensor_tensor(out=x[:, :, 0:T], in0=x[:, :, 0:T], in1=validbc[:, mask_out:mask_out + T].unsqueeze(1).broadcast_to([128, 8, T]), op=ALU.mult),
                ['x%d' % k for k in range(8)] + ['validbc'], ['x%d' % k for k in range(8)])
        if yout is not None:
            for gi in range(ng):
                st = next_stage()
                for hb in range(2):
                    b = next_pp()
                    for jj in range(4):
                        j = hb * 4 + jj
                        tr(bank[b][:, jj * 128:(jj + 1) * 128], x[:, j, gi * 128:(gi + 1) * 128], ident[:], ['x%d' % j, 'ident'], ['bank%d' % b])
                    dve(lambda e, b=b, st=st, hb=hb: e.tensor_copy(out=stage[st][:, hb * 512:(hb + 1) * 512], in_=bank[b][:, 0:512]),
                        ['bank%d' % b], ['stage%d' % st])
                dma('sp', yout(gi), stage[st][:], 'stage%d' % st, reads=['stage%d' % st])

    def norm_o(typ, hc, Ov, ob, Odst, ou, l):
        ri = ob - 5
        ru = 'rec%d' % ri
        if typ == 'b':
            dve(lambda e: e.tensor_scalar_max(out=rec[ri][:], in0=Ov[:, :, 64], scalar1=1e-30), ['bank%d' % ob], [ru])
        else:
            dve(lambda e: e.tensor_tensor(out=rec[ri][:], in0=Ov[:, :, 64], in1=esink[:, l, hc * 4:(hc + 1) * 4], op=ALU.add),
                ['bank%d' % ob, 'esink'], [ru])
        dve(lambda e: e.reciprocal(out=rec[ri][:], in_=rec[ri][:]), [ru], [ru])
        dve(lambda e: e.tensor_tensor(
            out=Odst[:, hc * 256:(hc + 1) * 256].rearrange("p (h d) -> p h d", d=64), in0=Ov[:, :, 0:64],
            in1=rec[ri][:, 0:4].unsqueeze(2).broadcast_to([128, 4, 64]), op=ALU.mult),
            ['bank%d' % ob, ru], [ou])

    def load_x(src_fn, ng):
        for gi in range(ng):
            st = next_stage()
            dma('sp', stage[st][:], src_fn(gi), 'stage%d' % st, writes=['stage%d' % st])
            for hb in range(2):
                b = next_pp()
                for jj in range(4):
                    j = hb * 4 + jj
                    tr(bank[b][:, jj * 128:(jj + 1) * 128], stage[st][:, j * 128:(j + 1) * 128], ident[:], ['stage%d' % st, 'ident'], ['bank%d' % b])
                dve(lambda e, b=b, hb=hb, gi=gi: e.tensor_copy(out=x[:, hb * 4:(hb + 1) * 4, gi * 128:(gi + 1) * 128],
                                                              in_=bank[b][:, 0:512].rearrange("p (j t) -> p j t", t=128)),
                    ['bank%d' % b], ['x%d' % j for j in range(hb * 4, hb * 4 + 4)])

    def conv_out(l, dst, key):
        for r_ in range(2):
            b_ = next_pp()
            tr(bank[b_][0:44, 0:128], cprev[:, l, :, r_], ident[:], ['cprev%d' % l, 'ident'], ['bank%d' % b_])
            st = next_stage()
            dve(lambda e, b_=b_, st=st: e.tensor_copy(out=stage[st][0:44, 0:128], in_=bank[b_][0:44, 0:128]), ['bank%d' % b_], ['stage%d' % st])
            dma('sp', dst[l, r_].rearrange("(k p) -> k p", p=128), stage[st][0:44, 0:128], 'stage%d' % st, reads=['stage%d' % st])

    NT_DBG = int(os.environ.get('KDBG_NT', '99'))
    L1_DBG = os.environ.get('KDBG_L1')
    for (g0, ng, k1, k2) in TILES[:NT_DBG]:
        if L1_DBG is not None:
            k1, k2 = L1_DBG, None
        load_x(lambda gi, g0=g0: xin[(g0 + gi) * 128:(g0 + gi + 1) * 128, :], ng)
        last = (g0 == 21)
        kvo = None
        if last:
            kvo = dict(a=lambda g: (True if g == 24 else None), ak=[o_ak[0], o_ak[1]], av=[o_av[0], o_av[1]],
                       bk=lambda l, g: o_bk[l, (g - 21) * 128:(g - 20) * 128, :], bv=lambda l, g: o_bv[l, (g - 21) * 128:(g - 20) * 128, :])
        layer(0, g0, ng, k1, kvout=kvo, mask_out={4: 0, 5: 128}.get(g0))
        if k2 == 'tail':
            layer(1, g0, ng, 'full', qlo=(ng - 1) * 128, skip_down=True)
        elif k2 is not None:
            layer(1, g0, ng, k2, kvout=kvo,
                  yout=(lambda gi, g0=g0: y_o[(g0 - G_OWN + gi) * 128:(g0 - G_OWN + gi + 1) * 128, :]) if k2 == 'full' else None)
    for l in range(2):
        conv_out(l, o_pc, 'pcout')

    for l in (range(2) if NT_DBG > 50 else ()):
        L = str(l)
        for r_ in range(2):
            load_T(cconv[l, r_].rearrange("(k p) -> k p", p=128), 44, cprev[:, l, :, r_], ['cprev' + L])
        for cg in range(4):
            slot = (GS - 4 + cg) % RB
            st = next_stage()
            dma('sp', stage[st][:], cbk[l, cg * 128:(cg + 1) * 128, :], 'stage%d' % st, writes=['stage%d' % st])
            for hb in range(2):
                b = next_pp()
                for jj in range(4):
                    j = hb * 4 + jj
                    tr(bank[b][:, jj * 128:(jj + 1) * 128], stage[st][:, j * 128:(j + 1) * 128], ident[:], ['stage%d' % st, 'ident'], ['bank%d' % b])
                dve(lambda e, b=b, hb=hb, slot=slot, l=l: e.tensor_copy(out=kbT[l][:, hb * 4:(hb + 1) * 4, slot * 128:(slot + 1) * 128],
                                                                       in_=bank[b][:, 0:512].rearrange("p (j t) -> p j t", t=128)),
                    ['bank%d' % b], ['kbT' + L])
            st = next_stage()
            dma('sp', stage[st][:], cbv[l, cg * 128:(cg + 1) * 128, :], 'stage%d' % st, writes=['stage%d' % st])
            dve(lambda e, st=st, slot=slot, l=l: e.tensor_copy(out=vb[l][:, slot, :, 0:64], in_=stage[st][:].rearrange("p (h d) -> p h d", d=64)),
                ['stage%d' % st], ['vb' + L])
            dve(lambda e, slot=slot, l=l: e.tensor_copy(out=vb[l][:, slot, :, 64], in_=ones16[:, 0:16]), ['ones16'], ['vb' + L])
        slot = (GS - 1) % RA
        st = next_stage()
        dma('sp', stage[st][:, 0:256], cak[l], 'stage%d' % st, writes=['stage%d' % st])
        b = next_pp()
        for j in range(2):
            tr(bank[b][:, j * 128:(j + 1) * 128], stage[st][:, j * 128:(j + 1) * 128], ident[:], ['stage%d' % st, 'ident'], ['bank%d' % b])
        for j in range(2):
            sc_ = slice(slot * 128, (slot + 1) * 128)
            dve(lambda e, b=b, j=j, sc_=sc_, l=l: e.tensor_copy(out=kaT[l][0:64, 2 * j, sc_], in_=bank[b][0:64, j * 128:(j + 1) * 128]), ['bank%d' % b], ['kaT' + L])
            dve(lambda e, b=b, j=j, sc_=sc_, l=l: e.tensor_copy(out=kaT[l][64:128, 2 * j + 1, sc_], in_=bank[b][64:128, j * 128:(j + 1) * 128]), ['bank%d' % b], ['kaT' + L])
            dma('sp', kaT[l][64:128, 2 * j, sc_], kaT[l][0:64, 2 * j, sc_], 'kadup0', reads=['kaT' + L], writes=['kaT' + L])
            dma('sp', kaT[l][0:64, 2 * j + 1, sc_], kaT[l][64:128, 2 * j + 1, sc_], 'kadup1', reads=['kaT' + L], writes=['kaT' + L])
        st = next_stage()
        dma('sp', stage[st][:, 0:256], cav[l], 'stage%d' % st, writes=['stage%d' % st])
        dve(lambda e, st=st, slot=slot, l=l: e.tensor_copy(out=va[l][:, slot, :, 0:64], in_=stage[st][:, 0:256].rearrange("p (h d) -> p h d", d=64)),
            ['stage%d' % st], ['va' + L])
        dve(lambda e, slot=slot, l=l: e.tensor_copy(out=va[l][:, slot, :, 64], in_=ones16[:, 0:4]), ['ones16'], ['va' + L])
    if NT_DBG <= 50:
        P.emit(final_dma_keys=[k for k in ('stage0', 'stage1') if k in P.dma_count])
        return nc
    load_x(lambda gi: xs_in, 1)
    kvo = dict(a=lambda g: True, ak=[o_sak[0], o_sak[1]], av=[o_sav[0], o_sav[1]],
               bk=lambda l, g: o_sbk[l], bv=lambda l, g: o_sbv[l])
    layer(0, GS, 1, 'full', kvout=kvo, tlast=32)
    layer(1, GS, 1, 'full', kvout=kvo, tlast=32, yout=lambda gi: ys_o)
    for l in range(2):
        conv_out(l, o_sc, 'scout')

    P.emit(final_dma_keys=['stage0', 'stage1'])
    return nc


_CACHE = {}


def _consts():
    ident = np.eye(128, dtype=np.float32)
    jflip = np.ascontiguousarray(ident[::-1])
    ones = np.ones((128, 128), np.float32)
    blk = np.zeros((128, 128), np.float32)
    blk[:64, :64] = 1.0
    blk[64:, 64:] = 1.0
    k = np.arange(128)[:, None]
    q = np.arange(128)[None, :]
    alibi = np.zeros((128, 2, 128), np.float32)
    BIG = 3.0e7
    d0 = np.abs(128 + q - k).astype(np.float32)
    d0 = np.where((k < 64) & (q >= 64), BIG, d0)
    d1 = np.abs(q - k).astype(np.float32)
    d1 = np.where((k >= 64) & (q < 64), BIG, d1)
    alibi[:, 0, :] = d0
    alibi[:, 1, :] = d1
    return dict(ident=ident, jflip=jflip, ones=ones, blockones=blk, alibiD=alibi)


def kernel(x_prompt, x_sample, cache_a_k, cache_a_v, cache_b_k, cache_b_v, cache_ffn_conv,
           norm1_g, w_in, b_gate, qn_a_g, kn_a_g, qn_b_g, kn_b_g, sinks_a, rel_bias_b,
           w_out, norm2_g, w_up, conv_w, conv_b, w_down):
    f = lambda a: np.ascontiguousarray(np.asarray(a, dtype=np.float32))
    x_prompt = f(x_prompt); x_sample = f(x_sample)
    if 'nc' not in _CACHE:
        _CACHE['nc'] = build_program()
    nc = _CACHE['nc']
    consts = _consts()
    shared = dict(norm1_g=f(norm1_g), w_in=f(w_in), b_gate=f(b_gate), qn_a_g=f(qn_a_g), kn_a_g=f(kn_a_g),
                  qn_b_g=f(qn_b_g), kn_b_g=f(kn_b_g), sinks_a=f(sinks_a), rel_bias_b=f(rel_bias_b), w_out=f(w_out),
                  norm2_g=f(norm2_g), w_up=f(w_up), conv_w=f(conv_w), conv_b=f(conv_b), w_down=f(w_down), **consts)
    HALO = 18 * 64
    in_maps = []
    for c in range(8):
        b, half = c // 2, c % 2
        start = half * 2048
        xin = np.zeros((NGP * 128, D), np.float32)
        lo = start - HALO
        s0 = max(lo, 0)
        xin[s0 - lo:, :] = x_prompt[b, s0:start + 2048, :]
        pos = lo + np.arange(NGP * 128)
        vmask = (pos >= 0).astype(np.float32)
        valid = np.zeros((128, 32), np.float32)
        valid[:, :NGP] = vmask.reshape(NGP, 128).T
        valid[:32, GS] = 1.0
        validbc = np.ascontiguousarray(np.broadcast_to(vmask[4 * 128:9 * 128][None, :], (128, 640))).astype(np.float32)
        xs = np.zeros((128, D), np.float32)
        xs[:32] = x_sample[c]
        m = dict(shared)
        m.update(xin=xin, xs=xs, valid=valid, validbc=validbc,
                 cak=f(np.asarray(cache_a_k)[:, c].reshape(2, 128, 256)), cav=f(np.asarray(cache_a_v)[:, c].reshape(2, 128, 256)),
                 cbk=f(np.asarray(cache_b_k)[:, c].reshape(2, 512, 1024)), cbv=f(np.asarray(cache_b_v)[:, c].reshape(2, 512, 1024)),
                 cconv=f(np.asarray(cache_ffn_conv)[:, c]))
        in_maps.append(m)
    res = run_bass_kernel_spmd(nc, in_maps, core_ids=list(range(8)))
    r = res.results
    y_prompt = np.stack([np.concatenate([r[2 * b]["y"], r[2 * b + 1]["y"]], axis=0) for b in range(4)])
    y_sample = np.stack([r[c]["ys"][:32] for c in range(8)])
    odd = [1, 3, 5, 7]
    pak = np.stack([r[c]["o_ak"] for c in odd], axis=1).reshape(2, 4, 128, 4, 64)
    pav = np.stack([r[c]["o_av"] for c in odd], axis=1).reshape(2, 4, 128, 4, 64)
    pbk = np.stack([r[c]["o_bk"] for c in odd], axis=1).reshape(2, 4, 512, 16, 64)
    pbv = np.stack([r[c]["o_bv"] for c in odd], axis=1).reshape(2, 4, 512, 16, 64)
    pcv = np.stack([r[c]["o_pc"] for c in odd], axis=1)
    sak = np.stack([r[c]["o_sak"][:, :32] for c in range(8)], axis=1).reshape(2, 8, 32, 4, 64)
    sav = np.stack([r[c]["o_sav"][:, :32] for c in range(8)], axis=1).reshape(2, 8, 32, 4, 64)
    sbk = np.stack([r[c]["o_sbk"][:, :32] for c in range(8)], axis=1).reshape(2, 8, 32, 16, 64)
    sbv = np.stack([r[c]["o_sbv"][:, :32] for c in range(8)], axis=1).reshape(2, 8, 32, 16, 64)
    scv = np.stack([r[c]["o_sc"] for c in range(8)], axis=1)
    outs = (y_prompt, y_sample, pak, pav, pbk, pbv, pcv, sak, sav, sbk, sbv, scv)
    return tuple(np.ascontiguousarray(o, dtype=np.float32) for o in outs)
```
